# Optimizing a Trainium2 kernel written in Bass

```python
import math
import jax
import jax.numpy as jnp
from jax import lax
import numpy as np

D_MODEL = 2048
BATCH = 4
SEQ = 2048
DEPTH = 1
DEC_BATCH = 128
DEC_SEQ = 4
PAST_LEN = 16384
PAGE_SIZE = 128

MIX_WIDTH = D_MODEL
W_A = MIX_WIDTH // 2
W_B = MIX_WIDTH - W_A
HGRN_EXPAND = 128
H_A = W_A // HGRN_EXPAND
DK = HGRN_EXPAND
DV = W_A // H_A
CHUNK = 64
S5_GROUP = 16
G_B = W_B // S5_GROUP
P_STATE = 64
D_FF = 4 * D_MODEL
IN_COLS = 4 * W_A + W_B
EPS = 1e-6

kernel_name = "hybrid_hgrn2_s5_decode_step"


def rmsnorm(x, g):
    xf = x.astype(jnp.float32)
    var = jnp.mean(xf * xf, axis=-1, keepdims=True)
    return xf * lax.rsqrt(var + EPS) * g.astype(jnp.float32)


def hgrn2_recurrence(q, k, v, logf, s0):
    b, t = q.shape[0], q.shape[1]
    c = min(CHUNK, t)
    n = -(-t // c)
    pad = n * c - t
    if pad:
        padw = ((0, 0), (0, pad), (0, 0), (0, 0))
        q, k, v, logf = [jnp.pad(a, padw) for a in (q, k, v, logf)]

    def to_chunks(a):
        return jnp.moveaxis(a.reshape(b, n, c, a.shape[2], a.shape[3]), 1, 0)

    qc, kc, vc, gc = map(to_chunks, (q, k, v, logf))
    causal = jnp.tril(jnp.ones((c, c), dtype=bool))[None, :, :, None, None]

    def step(s, inp):
        qi, ki, vi, gi = inp
        bcum = jnp.cumsum(gi, axis=1)
        o_inter = jnp.einsum('bthk,bhkv->bthv', qi * jnp.exp(bcum), s)
        diff = bcum[:, :, None] - bcum[:, None, :]
        decay = jnp.exp(jnp.where(causal, diff, -jnp.inf))
        scores = jnp.einsum('bthk,bshk,btshk->bhts', qi, ki, decay)
        o_intra = jnp.einsum('bhts,bshv->bthv', scores, vi)
        b_last = bcum[:, -1]
        k_dec = ki * jnp.exp(b_last[:, None] - bcum)
        s_new = jnp.exp(b_last)[..., None] * s + jnp.einsum('bshk,bshv->bhkv', k_dec, vi)
        return s_new, o_inter + o_intra

    s_final, oc = lax.scan(step, s0, (qc, kc, vc, gc))
    o = jnp.moveaxis(oc, 0, 1).reshape(b, n * c, H_A, DV)[:, :t]
    return o, s_final


def hgrn2_mixer(q_pre, f_pre, i_pre, g_pre, lb, norm_g, s0):
    bsz, t, _ = q_pre.shape
    q = jax.nn.silu(q_pre)
    f = lb + (1.0 - lb) * jax.nn.sigmoid(f_pre)
    k = 1.0 - f
    logf = jnp.log(f)
    o, s_new = hgrn2_recurrence(q.reshape(bsz, t, H_A, DK), k.reshape(bsz, t, H_A, DK),
                                i_pre.reshape(bsz, t, H_A, DV), logf.reshape(bsz, t, H_A, DK), s0)
    o = rmsnorm(o.reshape(bsz, t, W_A), norm_g) * jax.nn.silu(g_pre)
    return o, s_new


def s5_mixer(u, a_re, a_im, b_re, b_im, c_re, c_im, d, log_step, glu_w, glu_b, x0_re, x0_im):
    bsz, t, _ = u.shape
    f32 = jnp.float32
    a_re = a_re.astype(f32)
    a_im = a_im.astype(f32)
    dt = jnp.exp(log_step.astype(f32))[:, None]
    mag = jnp.exp(a_re * dt)
    lam_re, lam_im = mag * jnp.cos(a_im * dt), mag * jnp.sin(a_im * dt)
    den = a_re * a_re + a_im * a_im
    nr, ni = lam_re - 1.0, lam_im
    r_re = (nr * a_re + ni * a_im) / den
    r_im = (ni * a_re - nr * a_im) / den
    b_re = b_re.astype(f32)
    b_im = b_im.astype(f32)
    bb_re = r_re[..., None] * b_re - r_im[..., None] * b_im
    bb_im = r_re[..., None] * b_im + r_im[..., None] * b_re
    ug = u.reshape(bsz, t, G_B, S5_GROUP)
    bu_re = jnp.einsum('btgc,gpc->btgp', ug, bb_re)
    bu_im = jnp.einsum('btgc,gpc->btgp', ug, bb_im)
    bu_re = bu_re.at[:, 0].add(lam_re * x0_re - lam_im * x0_im)
    bu_im = bu_im.at[:, 0].add(lam_re * x0_im + lam_im * x0_re)
    a_el_re = jnp.broadcast_to(lam_re, bu_re.shape)
    a_el_im = jnp.broadcast_to(lam_im, bu_im.shape)

    def combine(e1, e2):
        a1r, a1i, b1r, b1i = e1
        a2r, a2i, b2r, b2i = e2
        return (a2r * a1r - a2i * a1i, a2r * a1i + a2i * a1r,
                a2r * b1r - a2i * b1i + b2r, a2r * b1i + a2i * b1r + b2i)

    _, _, xr, xi = lax.associative_scan(combine, (a_el_re, a_el_im, bu_re, bu_im), axis=1)
    y = (jnp.einsum('btgp,gcp->btgc', xr, c_re.astype(f32))
         - jnp.einsum('btgp,gcp->btgc', xi, c_im.astype(f32)))
    y = y.reshape(bsz, t, W_B) + d.astype(f32) * u
    y = jax.nn.gelu(y)
    y = y * jax.nn.sigmoid(y @ glu_w.astype(f32) + glu_b.astype(f32))
    return y, xr[:, -1], xi[:, -1]


def trunk_layer(x, s_hgrn0, x0_re, x0_im, lb, w_in, w_out, norm1_g, norm2_g, hgrn_norm_g,
                s5_a_re, s5_a_im, s5_b_re, s5_b_im, s5_c_re, s5_c_im, s5_d, s5_log_step,
                glu_w, glu_b, mlp_up, mlp_down):
    h = rmsnorm(x, norm1_g)
    proj = h @ w_in.astype(jnp.float32)
    q_pre, f_pre, i_pre, g_pre, u = jnp.split(proj, [W_A, 2 * W_A, 3 * W_A, 4 * W_A], axis=-1)
    o_a, s_a = hgrn2_mixer(q_pre, f_pre, i_pre, g_pre, lb, hgrn_norm_g, s_hgrn0.astype(jnp.float32))
    o_b, xr, xi = s5_mixer(u, s5_a_re, s5_a_im, s5_b_re, s5_b_im, s5_c_re, s5_c_im, s5_d,
                           s5_log_step, glu_w, glu_b,
                           x0_re.astype(jnp.float32), x0_im.astype(jnp.float32))
    mix = jnp.concatenate([o_a, o_b], axis=-1) @ w_out.astype(jnp.float32)
    x = x + mix.astype(x.dtype)
    h2 = rmsnorm(x, norm2_g)
    ff = jnp.square(jax.nn.relu(h2 @ mlp_up.astype(jnp.float32))) @ mlp_down.astype(jnp.float32)
    x = x + ff.astype(x.dtype)
    return x, s_a, xr, xi


def setup_inputs(seed: int = 0) -> dict:
    key = jax.random.key(seed)
    ks = jax.random.split(key, 32)
    f32 = jnp.float32
    nrm = lambda k, shape, s: jax.random.normal(k, shape, f32) * s
    n_idx = jnp.arange(P_STATE, dtype=f32)
    return {
        "x_prompt": nrm(ks[0], (BATCH, SEQ, D_MODEL), 1.0),
        "x_sample": nrm(ks[1], (DEC_BATCH, DEC_SEQ, D_MODEL), 1.0),
        "state_hgrn": nrm(ks[2], (DEPTH, DEC_BATCH, H_A, DK, DV), 0.5),
        "state_s5_re": nrm(ks[3], (DEPTH, DEC_BATCH, G_B, P_STATE), 0.1),
        "state_s5_im": nrm(ks[4], (DEPTH, DEC_BATCH, G_B, P_STATE), 0.1),
        "w_in": nrm(ks[5], (DEPTH, D_MODEL, IN_COLS), D_MODEL ** -0.5),
        "w_out": nrm(ks[6], (DEPTH, MIX_WIDTH, D_MODEL), MIX_WIDTH ** -0.5),
        "norm1_g": 1.0 + nrm(ks[7], (DEPTH, D_MODEL), 0.01),
        "norm2_g": 1.0 + nrm(ks[8], (DEPTH, D_MODEL), 0.01),
        "hgrn_lb_logits": nrm(ks[9], (DEPTH + 1, W_A), 0.1),
        "hgrn_norm_g": 1.0 + nrm(ks[10], (DEPTH, W_A), 0.01),
        "s5_a_re": -0.5 + nrm(ks[11], (DEPTH, G_B, P_STATE), 0.01),
        "s5_a_im": math.pi * n_idx + nrm(ks[12], (DEPTH, G_B, P_STATE), 0.01),
        "s5_b_re": nrm(ks[13], (DEPTH, G_B, P_STATE, S5_GROUP), (2 * S5_GROUP) ** -0.5),
        "s5_b_im": nrm(ks[14], (DEPTH, G_B, P_STATE, S5_GROUP), (2 * S5_GROUP) ** -0.5),
        "s5_c_re": nrm(ks[15], (DEPTH, G_B, S5_GROUP, P_STATE), P_STATE ** -0.5),
        "s5_c_im": nrm(ks[16], (DEPTH, G_B, S5_GROUP, P_STATE), P_STATE ** -0.5),
        "s5_d": nrm(ks[17], (DEPTH, W_B), 1.0),
        "s5_log_step": jax.random.uniform(ks[18], (DEPTH, G_B), f32, math.log(0.001), math.log(0.1)),
        "glu_w": nrm(ks[19], (DEPTH, W_B, W_B), W_B ** -0.5),
        "glu_b": nrm(ks[20], (DEPTH, W_B), 0.01),
        "mlp_up": nrm(ks[21], (DEPTH, D_MODEL, D_FF), D_MODEL ** -0.5),
        "mlp_down": nrm(ks[22], (DEPTH, D_FF, D_MODEL), D_FF ** -0.5),
        "final_norm_g": 1.0 + nrm(ks[23], (D_MODEL,), 0.01),
    }


def reference(x_prompt, x_sample, state_hgrn, state_s5_re, state_s5_im, w_in, w_out, norm1_g, norm2_g,
              hgrn_lb_logits, hgrn_norm_g, s5_a_re, s5_a_im, s5_b_re, s5_b_im, s5_c_re, s5_c_im, s5_d,
              s5_log_step, glu_w, glu_b, mlp_up, mlp_down, final_norm_g):
    f32 = jnp.float32
    lb_all = jnp.cumsum(jax.nn.softmax(hgrn_lb_logits.astype(f32), axis=0), axis=0)
    bp = x_prompt.shape[0]
    xp, xs = x_prompt, x_sample
    hp_list, rp_list, ip_list, hs_list, rs_list, is_list = [], [], [], [], [], []
    for l in range(DEPTH):
        shared = (lb_all[l], w_in[l], w_out[l], norm1_g[l], norm2_g[l], hgrn_norm_g[l],
                  s5_a_re[l], s5_a_im[l], s5_b_re[l], s5_b_im[l], s5_c_re[l], s5_c_im[l], s5_d[l],
                  s5_log_step[l], glu_w[l], glu_b[l], mlp_up[l], mlp_down[l])
        xp, sh_p, sr_p, si_p = trunk_layer(
            xp, jnp.zeros((bp, H_A, DK, DV), f32), jnp.zeros((bp, G_B, P_STATE), f32),
            jnp.zeros((bp, G_B, P_STATE), f32), *shared)
        xs, sh_s, sr_s, si_s = trunk_layer(xs, state_hgrn[l], state_s5_re[l], state_s5_im[l], *shared)
        hp_list.append(sh_p); rp_list.append(sr_p); ip_list.append(si_p)
        hs_list.append(sh_s); rs_list.append(sr_s); is_list.append(si_s)
    y_prompt = rmsnorm(xp, final_norm_g).astype(x_prompt.dtype)
    y_sample = rmsnorm(xs, final_norm_g).astype(x_sample.dtype)
    new_hgrn_prompt = jnp.stack(hp_list)
    new_s5_re_prompt = jnp.stack(rp_list)
    new_s5_im_prompt = jnp.stack(ip_list)
    new_hgrn_sample = jnp.stack(hs_list)
    new_s5_re_sample = jnp.stack(rs_list)
    new_s5_im_sample = jnp.stack(is_list)
    return (y_prompt, y_sample, new_hgrn_prompt, new_s5_re_prompt, new_s5_im_prompt,
            new_hgrn_sample, new_s5_re_sample, new_s5_im_sample)
```

```python
from contextlib import ExitStack
import math
import os
import numpy as np
import concourse.bass as bass
import concourse.mybir as mybir
from concourse.bass_utils import run_bass_kernel_spmd

F32 = mybir.dt.float32
BF16 = mybir.dt.bfloat16
AF = mybir.ActivationFunctionType
ALU = mybir.AluOpType

D = 2048
KC = 16
NPB = 512
NSB = 8
NS_TOK = 4 * NSB
TB = NPB + NS_TOK
NBLK = 2
NPRE = 2
EPS = 1e-6
DFF = 8192
HG = 4
TWO_PI = 2.0 * math.pi
MAGIC = 12582912.0


class Buf:
    def __init__(self, t, name):
        self.t = t
        self.name = name
        self.w = None
        self.r = {}
        self.dsem = None
        self.dcnt = 0

    def __getitem__(self, k):
        return self.t[k]


class Eng:
    def __init__(self, e, sem, name):
        self.e = e
        self.sem = sem
        self.cnt = 0
        self.name = name
        self.waited = {}


class KB:
    def __init__(self, nc, es):
        self.nc = nc
        self.es = es
        self.nsem = 0
        self.PE = self.mk(nc.tensor, "pe")
        self.ACT = self.mk(nc.scalar, "act")
        self.DVE = self.mk(nc.vector, "dve")
        self.POOL = self.mk(nc.gpsimd, "pool")
        self.SP = self.mk(nc.sync, "sp")
        self.out_tokens = {}

    def newsem(self, name):
        self.nsem += 1
        return self.es.enter_context(self.nc.semaphore(name))

    def mk(self, e, name):
        return Eng(e, self.newsem("s_" + name), name)

    def sb(self, name, shape, dt=F32, es=None):
        t = (es or self.es).enter_context(self.nc.sbuf_tensor(name, shape, dt))
        return Buf(t, name)

    def ps(self, name, shape, dt=F32, es=None):
        t = (es or self.es).enter_context(self.nc.psum_tensor(name, shape, dt))
        return Buf(t, name)

    def _deps(self, eng, reads, writes):
        deps = {}

        def add(tok):
            if tok is None:
                return
            s, v = tok
            if deps.get(id(s), (None, 0))[1] < v:
                deps[id(s)] = (s, v)
        for b in reads:
            add(b.w)
        for b in writes:
            add(b.w)
            for tok in b.r.values():
                add(tok)
        for s, v in deps.values():
            if s is eng.sem and (eng.name == "pe" or v > eng.cnt):
                continue
            if eng.waited.get(id(s), 0) < v:
                eng.e.wait_ge(s, v)
                eng.waited[id(s)] = v

    @staticmethod
    def _mark(tok, reads, writes):
        for b in reads:
            if b.r.get(id(tok[0]), (None, 0))[1] < tok[1]:
                b.r[id(tok[0])] = tok
        for b in writes:
            b.w = tok
            b.r = {}

    def op(self, eng, fn, reads=(), writes=(), inc=True):
        self._deps(eng, reads, writes)
        ins = fn()
        if inc:
            eng.cnt += 1
            ins.then_inc(eng.sem, 1)
            tok = (eng.sem, eng.cnt)
        else:
            tok = (eng.sem, eng.cnt + 1)
        self._mark(tok, reads, writes)
        return ins

    def dma(self, eng, out_ap, in_ap, reads=(), writes=(), **kw):
        self._deps(eng, reads, writes)
        owner = writes[0] if writes else reads[0]
        if owner.dsem is None:
            owner.dsem = self.newsem("d_" + owner.name)
        owner.dcnt += 16
        eng.e.dma_start(out=out_ap, in_=in_ap, **kw).then_inc(owner.dsem, 16)
        tok = (owner.dsem, owner.dcnt)
        self._mark(tok, reads, writes)
        self.out_tokens[id(tok[0])] = tok
        return tok


def build(nblk=NBLK, stage="full", npre=NPRE):
    nc = bass.Bass("TRN2", target_bir_lowering=False)
    NP = nblk * NPB
    NS = nblk * NSB
    dbg_nomlp = os.environ.get("DBG_NOMLP") == "1"
    dbg_npair = int(os.environ.get("DBG_NPAIR", "32"))

    def din(name, shape):
        return nc.dram_tensor(name, list(shape), F32, kind="ExternalInput").ap()

    def dout(name, shape):
        return nc.dram_tensor(name, list(shape), F32, kind="ExternalOutput").ap()

    xp = din("xp", [NP, D])
    xpre = din("xpre", [max(npre, 1) * NPB, D])
    flag_in = din("flag", [128, 1])
    xs = din("xs", [NS * 4, D])
    w_in = din("w_in", [D, 5120])
    w_out = din("w_out", [D, D])
    g1 = din("g1", [1, D])
    g2 = din("g2", [1, D])
    gf = din("gf", [1, D])
    mlp_up = din("mlp_up", [D, DFF])
    mlp_dn = din("mlp_dn", [DFF, D])
    ident_in = din("ident", [128, 128])
    lbl_in = din("lbl", [2, 1024])
    hgn_in = din("hgn", [1, 1024])
    sh = din("sh", [NS, 8, 128, 128])
    maskP_in = din("maskP", [64, 64])
    maskS_in = din("maskS", [NS_TOK, NS_TOK])
    cmask_in = din("cmask", [1, TB])
    seqm_in = din("seqm", [NS_TOK, NSB])
    are_in = din("a_re", [64, 64])
    aim_in = din("a_im", [64, 64])
    lst_in = din("lstep", [1, 64])
    bre_in = din("b_re", [64, 64, 16])
    bim_in = din("b_im", [64, 64, 16])
    cre_in = din("c_re", [64, 16, 64])
    cim_in = din("c_im", [64, 16, 64])
    sd_in = din("s5d", [1, 1024])
    gluw = din("glu_w", [1024, 1024])
    glub_in = din("glu_b", [1, 1024])
    sre = din("sre", [NS, 64, 64])
    sim_ = din("sim", [NS, 64, 64])
    tpos_in = din("tpos", [nblk + npre, TB])
    rmask_in = din("rmask", [1, TB])
    hm_in = din("hm", [128, 2])
    gm_in = din("gm", [128, 8])

    yp = dout("yp", [NP, D])
    ys = dout("ys", [NS * 4, D])
    hp = dout("hp", [8, 128, 128])
    hs = dout("hs", [NS, 8, 128, 128])
    rp = dout("rp", [64, 64])
    ip = dout("ip", [64, 64])
    rs_o = dout("rs", [NS, 64, 64])
    is_o = dout("is_", [NS, 64, 64])
    bdt_d = nc.dram_tensor("bdt_d", [32, 128, 2, 128], BF16, kind="Internal").ap()
    ct_d = nc.dram_tensor("ct_d", [32, 128, 2, 128], BF16, kind="Internal").ap()

    with ExitStack() as es:
        K = KB(nc, es)
        PE, ACT, DVE, POOL, SP = K.PE, K.ACT, K.DVE, K.POOL, K.SP
        V = nc.vector
        A = nc.scalar
        G = nc.gpsimd
        T = nc.tensor

        ident = K.sb("ident_sb", [128, 128], BF16)
        K.dma(POOL, ident[:], ident_in, writes=[ident])
        identf = K.sb("identf_sb", [128, 128], F32)
        K.dma(SP, identf[:], ident_in, writes=[identf])
        gb = K.sb("gb", [128, D], F32)
        gsrc = {"g1": g1, "g2": g2, "gf": gf}

        def load_gain(nm):
            K.dma(SP, gb[:], gsrc[nm].partition_broadcast(128), writes=[gb])

        xt = [K.sb(f"xt{i}", [128, D], F32) for i in range(5)]
        tile_n = [128, 128, 128, 128, NS_TOK]
        tile_off = [0, 128, 256, 384, 512]
        hT = K.sb("hT", [128, KC, TB], BF16)
        htmh = [K.sb(f"htm{i}", [128, D // 2], BF16) for i in range(2)]
        ss = K.sb("ss_sb", [128, 8], F32)
        rs = K.sb("rs_sb", [128, 8], F32)
        rstd = K.sb("rstd", [128, 8], F32)
        K.op(DVE, lambda: V.memset(ss[:], 1.0), writes=[ss])
        Fs = [K.sb(f"F{i}", [128, TB], F32) for i in range(9)]
        Bs = [K.sb(f"B{i}", [128, TB], BF16) for i in range(9)]
        oTh = [K.sb(f"oT{i}", [128, TB], F32) for i in range(8)]
        gTh = [K.sb(f"gT{i}", [128, TB], BF16) for i in range(8)]
        ybj = [K.sb(f"yb{i}", [128, TB], BF16) for i in range(8)]
        mixS = K.sb("mixS", [128, 8, TB], BF16)
        NWU = 2
        wu = [K.sb(f"wu{i}", [128, KC, 256], BF16) for i in range(NWU)]
        NWD = 5
        wd = [K.sb(f"wd{i}", [128, D], BF16) for i in range(NWD)]

        def rms_stats_all(junk_buf, junk_ap):
            for i in range(5):
                n = tile_n[i]
                K.op(ACT, lambda: A.activation(out=junk_ap[:n, :], in_=xt[i][:n, :], func=AF.Square,
                                               accum_out=ss[:n, i:i + 1]),
                     reads=[xt[i]], writes=[junk_buf, ss])
            K.op(DVE, lambda: V.tensor_scalar(out=rs[:, 0:5], in0=ss[:, 0:5],
                                              scalar1=1.0 / D, scalar2=EPS, op0=ALU.mult, op1=ALU.add),
                 reads=[ss], writes=[rs])
            K.op(ACT, lambda: A.activation(out=rs[:, 0:5], in_=rs[:, 0:5], func=AF.Sqrt), reads=[rs], writes=[rs])
            K.op(DVE, lambda: V.reciprocal(out=rstd[:, 0:5], in_=rs[:, 0:5]), reads=[rs], writes=[rstd])

        with ExitStack() as pes:
            tp = K.ps("tp", [128, 1024], BF16, es=pes)
            acc = [K.ps(f"acc{i}", [128, 1024], F32, es=pes) for i in range(2)]
            pA = K.ps("pA", [128, 512], F32, es=pes)
            pB = K.ps("pB", [128, 512], F32, es=pes)
            pC = K.ps("pC", [128, 512], F32, es=pes)
            ctr = {"wu": 0, "wd": 0, "acc": 0, "kv": 0, "s0": 0, "pr": 0}

            def next_acc():
                a = acc[ctr["acc"] % 2]
                ctr["acc"] += 1
                return a

            def next_wu():
                s_ = wu[ctr["wu"] % NWU]
                ctr["wu"] += 1
                return s_

            def next_wd():
                s_ = wd[ctr["wd"] % NWD]
                ctr["wd"] += 1
                return s_

            def norm_to_hT(gname):
                load_gain(gname)
                rms_stats_all(mixS, mixS[:].rearrange("p j t -> p (j t)")[:, 0:D])
                for i in range(5):
                    n = tile_n[i]
                    for half in range(2):
                        hb_ = htmh[half]
                        K.op(DVE, lambda: V.scalar_tensor_tensor(out=hb_[:n, :], in0=xt[i][:n, half * 1024:(half + 1) * 1024],
                                                                 scalar=rstd[:n, i:i + 1], in1=gb[:n, half * 1024:(half + 1) * 1024],
                                                                 op0=ALU.mult, op1=ALU.mult),
                             reads=[xt[i], rstd, gb], writes=[hb_])
                        for kc in range(8):
                            K.op(PE, lambda: T.transpose(out=tp[:, kc * 128:kc * 128 + n],
                                                         in_=hb_[:n, kc * 128:(kc + 1) * 128],
                                                         identity=ident[:n, :n]),
                                 reads=[hb_, ident], writes=[tp], inc=(kc == 7))
                        K.op(ACT, lambda: A.copy(out=hT[:, half * 8:half * 8 + 8, tile_off[i]:tile_off[i] + n],
                                                 in_=tp[:].rearrange("p (k c) -> p k c", c=128)[:, :, :n]),
                             reads=[tp], writes=[hT])

            def project(a, slot, nk=KC, rhs=None, c0=0):
                def lhs(kc):
                    if isinstance(slot, tuple):
                        sb_ = slot[kc // 8]
                        return sb_, sb_[:].rearrange("p (k c) -> p k c", c=256)[:, kc % 8, c0:c0 + 128]
                    return slot, slot[:, kc, c0:c0 + 128]
                for (lo, hi, last) in ((0, NPB, False), (NPB, TB, True)):
                    for kc in range(nk):
                        r_buf, r_ap = (hT, hT[:, kc, lo:hi]) if rhs is None else rhs(kc, lo, hi)
                        l_buf, l_ap = lhs(kc)
                        K.op(PE, lambda: T.matmul(a[:, lo:hi], lhsT=l_ap, rhs=r_ap,
                                                  start=(kc == 0), stop=(kc == nk - 1)),
                             reads=[l_buf, r_buf], writes=[a], inc=(last and kc == nk - 1))

            w_in_v = w_in.rearrange("(k p) c -> p k c", p=128)
            mlp_up_v = mlp_up.rearrange("(k p) c -> p k c", p=128)
            gluw_v = gluw.rearrange("(k p) c -> p k c", p=128)

            def proj_pair(base, h, slot):
                if h % 2 == 0:
                    src_ = w_in_v[:, :, base + h * 128:base + h * 128 + 256]
                    if isinstance(slot, tuple):
                        for hf_ in range(2):
                            K.dma(POOL, slot[hf_][:].rearrange("p (k c) -> p k c", c=256), src_[:, hf_ * 8:(hf_ + 1) * 8, :],
                                  writes=[slot[hf_]])
                    else:
                        K.dma(POOL, slot[:], src_, writes=[slot])
                a = next_acc()
                project(a, slot, c0=(h % 2) * 128)
                return a

            def proj_cols(col):
                slot = next_wu()
                K.dma(POOL, slot[:, :, 0:128], w_in_v[:, :, col:col + 128], writes=[slot])
                a = next_acc()
                project(a, slot)
                return a

            maskP = K.sb("maskP_sb", [64, 64], F32)
            K.dma(SP, maskP[:], maskP_in, writes=[maskP])
            maskS = K.sb("maskS_sb", [NS_TOK, NS_TOK], F32)
            K.dma(SP, maskS[:], maskS_in, writes=[maskS])
            cmask = K.sb("cmask_sb", [128, TB], F32)
            K.dma(SP, cmask[:], cmask_in.partition_broadcast(128), writes=[cmask])
            seqm = K.sb("seqm_sb", [NS_TOK, NSB], F32)
            K.dma(SP, seqm[:], seqm_in, writes=[seqm])
            lbl = K.sb("lbl_sb", [128, 2, 8], F32)
            K.dma(SP, lbl[:], lbl_in.rearrange("r (h k) -> k r h", k=128), writes=[lbl], allow_slow_non_contiguous=True)
            hgn = K.sb("hgn_sb", [128, 8], F32)
            K.dma(SP, hgn[:], hgn_in.rearrange("o (h k) -> k (o h)", k=128), writes=[hgn], allow_slow_non_contiguous=True)
            lb = K.sb("lb", [128, 8], F32)
            oml = K.sb("oml", [128, 8], F32)
            K.op(DVE, lambda: V.tensor_tensor(out=lb[:], in0=lbl[:, 0, :], in1=lbl[:, 1, :], op=ALU.subtract),
                 reads=[lbl], writes=[lb])
            K.op(ACT, lambda: A.activation(out=lb[:], in_=lb[:], func=AF.Sigmoid), reads=[lb], writes=[lb])
            K.op(DVE, lambda: V.tensor_scalar(out=oml[:], in0=lb[:], scalar1=-1.0, scalar2=1.0,
                                              op0=ALU.mult, op1=ALU.add), reads=[lb], writes=[oml])
            ones_b = K.sb("ones_b", [128, 128], BF16)
            onecol = K.sb("onecol", [128, 1], F32)
            K.op(DVE, lambda: V.memset(onecol[:], 1.0), writes=[onecol])
            K.op(DVE, lambda: V.memset(ones_b[:], 1.0), writes=[ones_b])

            t1, qf, ff, kf, lf, bc, ex, kde, rsb = Fs
            hsets = [tuple(Bs[0:4]), tuple(Bs[4:8])]
            sqb = Bs[8]
            ebls = [K.sb(f"ebl{i}", [128, 8 + NSB], F32) for i in range(2)]
            Sst = K.sb("Sst", [128, 8, 128], F32)
            Sb = K.sb("Sb", [128, 2, 128], BF16)
            kv = [K.sb(f"kv{i}", [128, 256], BF16) for i in range(2)]
            _scm = K.sb("scm0", [64, 64], BF16)
            scm = [_scm, _scm]
            S0all = K.sb("S0all", [128, NSB, 128], F32)
            s0v = [Buf(S0all.t[:, i, :], f"s0v{i}") for i in range(NSB)]

            def load_states(blk, h):
                K.dma(SP, S0all[:], sh[blk * NSB:(blk + 1) * NSB, h].rearrange("s k v -> k s v"), writes=[S0all] + s0v)

            def store_states(blk, h):
                K.dma(SP, hs[blk * NSB:(blk + 1) * NSB, h].rearrange("s k v -> k s v"), S0all[:], reads=[S0all] + s0v)
            _s0b = K.sb("s0b0", [128, 128], BF16)
            s0b = [_s0b, _s0b]
            _kdm = K.sb("kdm0", [NS_TOK, 128], BF16)
            kdm = [_kdm, _kdm]
            scp, opp, op2, Spp = pA, pB, pC, pC
            K.op(DVE, lambda: V.memset(Sst[:], 0.0), writes=[Sst])
            K.op(DVE, lambda: V.memset(Sb[:], 0.0), writes=[Sb])

            def hgrn_front(blk, h, full=True):
              qt, kt, kdT, vT = hsets[h % 2]
              ebl = ebls[h % 2]
              if full:
                a = proj_pair(0, h, (wd[0], wd[1]))
                K.op(ACT, lambda: A.activation(out=t1[:], in_=a[:, 0:TB], func=AF.Sigmoid), reads=[a], writes=[t1])
                K.op(DVE, lambda: V.tensor_tensor(out=qf[:], in0=a[:, 0:TB], in1=t1[:], op=ALU.mult),
                     reads=[a, t1], writes=[qf])
                yield
              if True:
                a = proj_pair(1024, h, wu[0])
                K.op(ACT, lambda: A.activation(out=t1[:], in_=a[:, 0:TB], func=AF.Sigmoid), reads=[a], writes=[t1])
                K.op(DVE, lambda: V.tensor_scalar(out=ff[:], in0=t1[:], scalar1=oml[:, h:h + 1], scalar2=lb[:, h:h + 1],
                                                  op0=ALU.mult, op1=ALU.add), reads=[t1, oml, lb], writes=[ff])
                K.op(DVE, lambda: V.tensor_scalar(out=kf[:], in0=ff[:], scalar1=-1.0, scalar2=1.0,
                                                  op0=ALU.mult, op1=ALU.add), reads=[ff], writes=[kf])
                K.op(ACT, lambda: A.activation(out=lf[:], in_=ff[:], func=AF.Ln), reads=[ff], writes=[lf])
                yield
                if full:
                    K.op(DVE, lambda: V.tensor_tensor_scan(out=bc[:], data0=cmask[:], data1=lf[:], initial=0.0,
                                                           op0=ALU.mult, op1=ALU.add), reads=[cmask, lf], writes=[bc])
                else:
                    K.op(DVE, lambda: V.tensor_tensor_scan(out=bc[:, 0:NPB], data0=onecol[:, 0:1].to_broadcast([128, NPB]),
                                                           data1=lf[:, 0:NPB], initial=0.0,
                                                           op0=ALU.mult, op1=ALU.add), reads=[onecol, lf], writes=[bc])
                if full:
                    K.op(ACT, lambda: A.activation(out=ex[:], in_=bc[:], func=AF.Exp), reads=[bc], writes=[ex])
                    K.op(DVE, lambda: V.tensor_tensor(out=qt[:], in0=qf[:], in1=ex[:], op=ALU.mult),
                         reads=[qf, ex], writes=[qt])
                    K.op(ACT, lambda: A.activation(out=ex[:], in_=bc[:], func=AF.Exp, scale=-1.0), reads=[bc], writes=[ex])
                    K.op(DVE, lambda: V.tensor_tensor(out=kt[:], in0=kf[:], in1=ex[:], op=ALU.mult),
                         reads=[kf, ex], writes=[kt])
                yield
                if not full:
                    K.op(ACT, lambda: A.activation(out=kde[:, 0:NPB], in_=bc[:, 0:NPB],
                                                   func=AF.Exp, bias=bc[:, NPB - 1:NPB], scale=-1.0),
                         reads=[bc], writes=[kde])
                if full:
                    bcp = bc[:, 0:NPB].rearrange("p (c t) -> p c t", t=64)
                    bcs_ = bc[:, NPB:TB].rearrange("p (c t) -> p c t", t=4)
                    K.op(DVE, lambda: V.tensor_tensor(out=kde[:, 0:NPB].rearrange("p (c t) -> p c t", t=64),
                                                      in0=bcp[:, :, 63:64].to_broadcast([128, 8, 64]), in1=bcp, op=ALU.subtract),
                         reads=[bc], writes=[kde])
                    K.op(DVE, lambda: V.tensor_tensor(out=kde[:, NPB:TB].rearrange("p (c t) -> p c t", t=4),
                                                      in0=bcs_[:, :, 3:4].to_broadcast([128, NSB, 4]), in1=bcs_, op=ALU.subtract),
                         reads=[bc], writes=[kde])
                    K.op(ACT, lambda: A.activation(out=kde[:], in_=kde[:], func=AF.Exp), reads=[kde], writes=[kde])
                wd_ = TB if full else NPB
                K.op(DVE, lambda: V.tensor_tensor(out=kdT[:, 0:wd_], in0=kf[:, 0:wd_], in1=kde[:, 0:wd_], op=ALU.mult),
                     reads=[kf, kde], writes=[kdT])
                if full:
                    K.op(ACT, lambda: A.activation(out=ebl[:, 0:8],
                                                   in_=bc[:, 0:NPB].rearrange("p (c t) -> p c t", t=64)[:, :, 63],
                                                   func=AF.Exp), reads=[bc], writes=[ebl])
                else:
                    K.op(ACT, lambda: A.activation(out=ebl[:, 0:1], in_=bc[:, NPB - 1:NPB], func=AF.Exp), reads=[bc], writes=[ebl])
                if full:
                    K.op(ACT, lambda: A.activation(out=ebl[:, 8:8 + NSB],
                                                   in_=bc[:, NPB:TB].rearrange("p (c t) -> p c t", t=4)[:, :, 3],
                                                   func=AF.Exp), reads=[bc], writes=[ebl])
                yield
                a = proj_pair(2048, h, wu[1])
                K.op(ACT, lambda: A.copy(out=vT[:], in_=a[:, 0:TB]), reads=[a], writes=[vT])
                yield
                if full:
                    a = proj_pair(3072, h, (wd[2], wd[3]))
                    K.op(ACT, lambda: A.activation(out=t1[:], in_=a[:, 0:TB], func=AF.Sigmoid), reads=[a], writes=[t1])
                    K.op(DVE, lambda: V.tensor_tensor(out=gTh[h][:], in0=a[:, 0:TB], in1=t1[:], op=ALU.mult),
                         reads=[a, t1], writes=[gTh[h]])
                yield

            def hgrn_chunks(blk, h, full=True):
                qt, kt, kdT, vT = hsets[h % 2]
                ebl = ebls[h % 2]
                if full:
                    K.op(ACT, lambda: A.copy(out=Sb[:, h % 2, :], in_=Sst[:, h, :]), reads=[Sst], writes=[Sb])
                oT = oTh[h]
                if not full:
                    for c in range(4):
                        kvb = kv[ctr["kv"] % 2]
                        ctr["kv"] += 1
                        o = 128 * c
                        K.op(PE, lambda: T.transpose(out=tp[:, 0:128], in_=kdT[:, o:o + 128], identity=ident[:, :]),
                             reads=[kdT, ident], writes=[tp], inc=False)
                        K.op(PE, lambda: T.transpose(out=tp[:, 128:256], in_=vT[:, o:o + 128], identity=ident[:, :]),
                             reads=[vT, ident], writes=[tp])
                        K.op(ACT, lambda: A.copy(out=kvb[:, :], in_=tp[:, 0:256]), reads=[tp], writes=[kvb])
                        K.op(PE, lambda: T.matmul(Spp[:, 128:256], lhsT=kvb[:, 0:128], rhs=kvb[:, 128:256],
                                                  start=(c == 0), stop=(c == 3)), reads=[kvb], writes=[Spp], inc=(c == 3))
                        yield
                    K.op(DVE, lambda: V.scalar_tensor_tensor(out=Sst[:, h, :], in0=Sst[:, h, :],
                                                             scalar=ebl[:, 0:1], in1=Spp[:, 128:256],
                                                             op0=ALU.mult, op1=ALU.add),
                         reads=[Sst, ebl, Spp], writes=[Sst])
                    yield
                    return
                for c in range(9 if full else 8):
                    if c:
                        yield
                    n = 64 if c < 8 else NS_TOK
                    o = 64 * c
                    kvb = kv[ctr["kv"] % 2]
                    sm = scm[ctr["kv"] % 2]
                    ctr["kv"] += 1
                    K.op(PE, lambda: T.transpose(out=tp[:n, 0:128], in_=kdT[:, o:o + n], identity=ident[:, :]),
                         reads=[kdT, ident], writes=[tp], inc=False)
                    K.op(PE, lambda: T.transpose(out=tp[:n, 128:256], in_=vT[:, o:o + n], identity=ident[:, :]),
                         reads=[vT, ident], writes=[tp])
                    K.op(ACT, lambda: A.copy(out=kvb[:n, :], in_=tp[:n, 0:256]), reads=[tp], writes=[kvb])
                    if full:
                        K.op(PE, lambda: T.matmul(scp[:n, :n], lhsT=kt[:, o:o + n], rhs=qt[:, o:o + n],
                                                  start=True, stop=True), reads=[kt, qt], writes=[scp])
                        mk = maskP if c < 8 else maskS
                        K.op(DVE, lambda: V.tensor_tensor(out=sm[:n, :n], in0=scp[:n, :n], in1=mk[:n, :n], op=ALU.mult),
                             reads=[scp, mk], writes=[sm])
                    if c < 8:
                        if full:
                            K.op(PE, lambda: T.matmul(opp[:, :n], lhsT=kvb[:n, 128:256], rhs=sm[:n, :n],
                                                      start=True, stop=False), reads=[kvb, sm], writes=[opp], inc=False)
                            K.op(PE, lambda: T.matmul(opp[:, :n], lhsT=Sb[:, h % 2, :], rhs=qt[:, o:o + n],
                                                      start=False, stop=True), reads=[Sb, qt], writes=[opp])
                            K.op(ACT, lambda: A.copy(out=oT[:, o:o + n], in_=opp[:, :n]), reads=[opp], writes=[oT])
                        K.op(PE, lambda: T.matmul(Spp[:, 128:256], lhsT=kvb[:n, 0:128], rhs=kvb[:n, 128:256],
                                                  start=True, stop=True), reads=[kvb], writes=[Spp])
                        K.op(DVE, lambda: V.scalar_tensor_tensor(out=Sst[:, h, :], in0=Sst[:, h, :],
                                                                 scalar=ebl[:, c:c + 1], in1=Spp[:, 128:256],
                                                                 op0=ALU.mult, op1=ALU.add),
                             reads=[Sst, ebl, Spp], writes=[Sst])
                        K.op(ACT, lambda: A.copy(out=Sb[:, h % 2, :], in_=Sst[:, h, :]), reads=[Sst], writes=[Sb])
                    else:
                        K.op(PE, lambda: T.matmul(opp[:, :n], lhsT=kvb[:n, 128:256], rhs=sm[:n, :n],
                                                  start=True, stop=True), reads=[kvb, sm], writes=[opp])
                        K.op(ACT, lambda: A.copy(out=oT[:, o:o + n], in_=opp[:, :n]), reads=[opp], writes=[oT])
                        for i in range(NSB):
                            sq = blk * NSB + i
                            j = ctr["s0"] % 2
                            ctr["s0"] += 1
                            K.op(ACT, lambda: A.copy(out=s0b[j][:], in_=s0v[i][:, :]), reads=[s0v[i]], writes=[s0b[j]])
                            K.op(PE, lambda: T.matmul(op2[:, 4 * i:4 * i + 4], lhsT=s0b[j][:, :],
                                                      rhs=qt[:, o + 4 * i:o + 4 * i + 4], start=True, stop=True),
                                 reads=[s0b[j], qt], writes=[op2])
                            K.op(DVE, lambda: V.tensor_scalar(out=kdm[j][:n, :], in0=kvb[:n, 0:128],
                                                              scalar1=seqm[:n, i:i + 1], scalar2=None, op0=ALU.mult),
                                 reads=[kvb, seqm], writes=[kdm[j]])
                            K.op(PE, lambda: T.matmul(Spp[:, 128:256], lhsT=kdm[j][:n, :], rhs=kvb[:n, 128:256],
                                                      start=True, stop=True), reads=[kdm[j], kvb], writes=[Spp])
                            K.op(DVE, lambda: V.scalar_tensor_tensor(out=s0v[i][:, :], in0=s0v[i][:, :],
                                                                     scalar=ebl[:, 8 + i:9 + i], in1=Spp[:, 128:256],
                                                                     op0=ALU.mult, op1=ALU.add),
                                 reads=[s0v[i], ebl, Spp], writes=[s0v[i]])
                        K.op(DVE, lambda: V.tensor_tensor(out=oT[:, o:o + n], in0=op2[:, :n], in1=oT[:, o:o + n],
                                                          op=ALU.add), reads=[op2, oT], writes=[oT])
                        store_states(blk, h)
                        if h < 7:
                            load_states(blk, h + 1)

                yield

            def run_streams(gens):
                gens = list(gens)
                while gens:
                    for g_ in list(gens):
                        try:
                            next(g_)
                        except StopIteration:
                            gens.remove(g_)

            def hgrn_all(blk, full, tail=None):
                if full:
                    load_states(blk, 0)
                run_streams([hgrn_front(blk, 0, full)])
                for h in range(8):
                    gs = [hgrn_chunks(blk, h, full)]
                    if h < 7:
                        gs.append(hgrn_front(blk, h + 1, full))
                    elif tail is not None:
                        gs.append(tail)
                    run_streams(gs)

            def hgrn_finish():
                a = next_acc()
                for h in range(8):
                    K.op(ACT, lambda: A.activation(out=sqb[:], in_=oTh[h][:], func=AF.Square), reads=[oTh[h]], writes=[sqb])
                    K.op(PE, lambda: T.matmul(a[:, 0:NPB], lhsT=ones_b[:], rhs=sqb[:, 0:NPB], start=(h == 0), stop=(h == 7)),
                         reads=[ones_b, sqb], writes=[a], inc=False)
                    K.op(PE, lambda: T.matmul(a[:, NPB:TB], lhsT=ones_b[:], rhs=sqb[:, NPB:TB], start=(h == 0), stop=(h == 7)),
                         reads=[ones_b, sqb], writes=[a])
                K.op(DVE, lambda: V.tensor_scalar(out=rsb[:], in0=a[:, 0:TB], scalar1=1.0 / 1024, scalar2=EPS,
                                                  op0=ALU.mult, op1=ALU.add), reads=[a], writes=[rsb])
                K.op(ACT, lambda: A.activation(out=rsb[:], in_=rsb[:], func=AF.Sqrt), reads=[rsb], writes=[rsb])
                K.op(DVE, lambda: V.reciprocal(out=rsb[:], in_=rsb[:]), reads=[rsb], writes=[rsb])
                for h in range(8):
                    K.op(DVE, lambda: V.scalar_tensor_tensor(out=t1[:], in0=oTh[h][:], scalar=hgn[:, h:h + 1],
                                                             in1=rsb[:], op0=ALU.mult, op1=ALU.mult),
                         reads=[oTh[h], hgn, rsb], writes=[t1])
                    K.op(DVE, lambda: V.tensor_tensor(out=gTh[h][:], in0=t1[:], in1=gTh[h][:], op=ALU.mult),
                         reads=[t1, gTh[h]], writes=[gTh[h]])

            w_out_v = w_out.rearrange("(c p) n -> p c n", p=128)
            opref = {}

            def load_cb(cb, which):
                if which == 0:
                    for hf in range(2):
                        K.dma(POOL, wu[hf][:], w_out_v[:, :, cb * 512 + hf * 256: cb * 512 + (hf + 1) * 256], writes=[wu[hf]])
                else:
                    for q4 in range(4):
                        K.dma(POOL, wd[q4][:].rearrange("p (c n) -> p c n", n=512),
                              w_out_v[:, 4 * q4:4 * q4 + 4, cb * 512:(cb + 1) * 512], writes=[wd[q4]])

            def rhs_of(cb, which, c, lo, hi):
                if which == 0:
                    hf = lo // 256
                    return wu[hf], wu[hf][:, c, lo - hf * 256:hi - hf * 256]
                q4 = c // 4
                return wd[q4], wd[q4][:].rearrange("p (c n) -> p c n", n=512)[:, c % 4, lo:hi]

            def out_proj(hook=None):
                if not opref.pop("cb0", False):
                    load_cb(0, 1)
                for cb in range(4):
                    which = 1 - cb % 2
                    if cb + 1 < 4:
                        load_cb(cb + 1, 1 - which)
                    for i in range(5):
                        n = tile_n[i]
                        o = tile_off[i]
                        ad = next_acc()
                        for hf in range(2):
                            for c in range(16):
                                mb_, ma_ = (gTh[c], gTh[c][:, o:o + n]) if c < 8 else (mixS, mixS[:, c - 8, o:o + n])
                                wb_, wa_ = rhs_of(cb, which, c, hf * 256, (hf + 1) * 256)
                                K.op(PE, lambda: T.matmul(ad[:n, hf * 256:(hf + 1) * 256], lhsT=ma_, rhs=wa_,
                                                          start=(c == 0), stop=(c == 15)),
                                     reads=[mb_, wb_], writes=[ad], inc=(hf == 1 and c == 15))
                        K.op(DVE, lambda: V.tensor_tensor(out=xt[i][:n, cb * 512:(cb + 1) * 512],
                                                          in0=ad[:n, 0:512], in1=xt[i][:n, cb * 512:(cb + 1) * 512],
                                                          op=ALU.add),
                             reads=[ad, xt[i]], writes=[xt[i]])
                if hook is not None:
                    hook()

            if stage == "full":
                PP = "(k g2) p -> (g2 p) k"
                are = K.sb("are", [128, 32], F32)
                aim = K.sb("aim", [128, 32], F32)
                lsp = K.sb("lsp", [128, 32], F32)
                K.dma(SP, are[:], are_in.rearrange(PP, g2=2), writes=[are], allow_slow_non_contiguous=True)
                K.dma(SP, aim[:], aim_in.rearrange(PP, g2=2), writes=[aim], allow_slow_non_contiguous=True)
                lsv = lst_in.rearrange("o (k g2) -> o g2 k", g2=2)
                for g2_ in range(2):
                    K.dma(SP, lsp[64 * g2_:64 * g2_ + 64, :], lsv[:, g2_, :].partition_broadcast(64), writes=[lsp],
                          allow_slow_non_contiguous=True)
                sp = {}
                for nm in ("dt", "mag", "th", "phif", "tmp", "tmp2", "sn", "cs", "lre", "lim", "den", "rre", "rim", "nr"):
                    sp[nm] = K.sb("s5_" + nm, [128, 32], F32)

                def s5op(eng, fn, reads, writes):
                    K.op(eng, fn, reads=[sp[r] if isinstance(r, str) else r for r in reads],
                         writes=[sp[w] if isinstance(w, str) else w for w in writes])
                s5op(ACT, lambda: A.activation(out=sp["dt"][:], in_=lsp[:], func=AF.Exp), [lsp], ["dt"])
                s5op(DVE, lambda: V.tensor_tensor(out=sp["tmp"][:], in0=are[:], in1=sp["dt"][:], op=ALU.mult), [are, "dt"], ["tmp"])
                s5op(ACT, lambda: A.activation(out=sp["mag"][:], in_=sp["tmp"][:], func=AF.Exp), ["tmp"], ["mag"])
                s5op(DVE, lambda: V.tensor_tensor(out=sp["th"][:], in0=aim[:], in1=sp["dt"][:], op=ALU.mult), [aim, "dt"], ["th"])
                s5op(DVE, lambda: V.tensor_scalar(out=sp["tmp"][:], in0=sp["th"][:], scalar1=1.0 / TWO_PI, scalar2=None,
                                                  op0=ALU.mult), ["th"], ["tmp"])
                s5op(DVE, lambda: V.tensor_scalar(out=sp["tmp2"][:], in0=sp["tmp"][:], scalar1=MAGIC, scalar2=None,
                                                  op0=ALU.add), ["tmp"], ["tmp2"])
                s5op(DVE, lambda: V.tensor_scalar(out=sp["tmp2"][:], in0=sp["tmp2"][:], scalar1=MAGIC, scalar2=None,
                                                  op0=ALU.subtract), ["tmp2"], ["tmp2"])
                s5op(DVE, lambda: V.tensor_tensor(out=sp["phif"][:], in0=sp["tmp"][:], in1=sp["tmp2"][:], op=ALU.subtract),
                     ["tmp", "tmp2"], ["phif"])
                s5op(ACT, lambda: A.activation(out=sp["sn"][:], in_=sp["phif"][:], func=AF.Sin, scale=TWO_PI), ["phif"], ["sn"])
                s5op(ACT, lambda: A.activation(out=sp["tmp"][:], in_=sp["phif"][:], func=AF.Abs), ["phif"], ["tmp"])
                s5op(DVE, lambda: V.tensor_scalar(out=sp["tmp"][:], in0=sp["tmp"][:], scalar1=-TWO_PI, scalar2=math.pi / 2,
                                                  op0=ALU.mult, op1=ALU.add), ["tmp"], ["tmp"])
                s5op(ACT, lambda: A.activation(out=sp["cs"][:], in_=sp["tmp"][:], func=AF.Sin), ["tmp"], ["cs"])
                s5op(DVE, lambda: V.tensor_tensor(out=sp["lre"][:], in0=sp["mag"][:], in1=sp["cs"][:], op=ALU.mult), ["mag", "cs"], ["lre"])
                s5op(DVE, lambda: V.tensor_tensor(out=sp["lim"][:], in0=sp["mag"][:], in1=sp["sn"][:], op=ALU.mult), ["mag", "sn"], ["lim"])
                s5op(DVE, lambda: V.tensor_tensor(out=sp["den"][:], in0=are[:], in1=are[:], op=ALU.mult), [are], ["den"])
                s5op(DVE, lambda: V.tensor_tensor(out=sp["tmp"][:], in0=aim[:], in1=aim[:], op=ALU.mult), [aim], ["tmp"])
                s5op(DVE, lambda: V.tensor_tensor(out=sp["den"][:], in0=sp["den"][:], in1=sp["tmp"][:], op=ALU.add), ["den", "tmp"], ["den"])
                s5op(DVE, lambda: V.reciprocal(out=sp["den"][:], in_=sp["den"][:]), ["den"], ["den"])
                s5op(DVE, lambda: V.tensor_scalar(out=sp["nr"][:], in0=sp["lre"][:], scalar1=-1.0, scalar2=None, op0=ALU.add),
                     ["lre"], ["nr"])
                s5op(DVE, lambda: V.tensor_tensor(out=sp["tmp"][:], in0=sp["nr"][:], in1=are[:], op=ALU.mult), ["nr", are], ["tmp"])
                s5op(DVE, lambda: V.tensor_tensor(out=sp["tmp2"][:], in0=sp["lim"][:], in1=aim[:], op=ALU.mult), ["lim", aim], ["tmp2"])
                s5op(DVE, lambda: V.tensor_tensor(out=sp["tmp"][:], in0=sp["tmp"][:], in1=sp["tmp2"][:], op=ALU.add), ["tmp", "tmp2"], ["tmp"])
                s5op(DVE, lambda: V.tensor_tensor(out=sp["rre"][:], in0=sp["tmp"][:], in1=sp["den"][:], op=ALU.mult), ["tmp", "den"], ["rre"])
                s5op(DVE, lambda: V.tensor_tensor(out=sp["tmp"][:], in0=sp["lim"][:], in1=are[:], op=ALU.mult), ["lim", are], ["tmp"])
                s5op(DVE, lambda: V.tensor_tensor(out=sp["tmp2"][:], in0=sp["nr"][:], in1=aim[:], op=ALU.mult), ["nr", aim], ["tmp2"])
                s5op(DVE, lambda: V.tensor_tensor(out=sp["tmp"][:], in0=sp["tmp"][:], in1=sp["tmp2"][:], op=ALU.subtract), ["tmp", "tmp2"], ["tmp"])
                s5op(DVE, lambda: V.tensor_tensor(out=sp["rim"][:], in0=sp["tmp"][:], in1=sp["den"][:], op=ALU.mult), ["tmp", "den"], ["rim"])
                phif, mag, lre, lim = sp["phif"], sp["mag"], sp["lre"], sp["lim"]

                hm = K.sb("hm_sb", [128, 2], F32)
                K.dma(SP, hm[:], hm_in, writes=[hm])
                gm = K.sb("gm_sb", [128, 8], F32)
                K.dma(SP, gm[:], gm_in, writes=[gm])
                gmn = K.sb("gmn_sb", [128, 8], F32)
                K.op(DVE, lambda: V.tensor_scalar(out=gmn[:], in0=gm[:], scalar1=-1.0, scalar2=None, op0=ALU.mult),
                     reads=[gm], writes=[gmn])
                dcol = K.sb("dcol", [128, 8], F32)
                K.dma(SP, dcol[:], sd_in.rearrange("o (j c) -> c (o j)", c=128), writes=[dcol], allow_slow_non_contiguous=True)
                glub = K.sb("glub", [128, 8], F32)
                K.dma(SP, glub[:], glub_in.rearrange("o (j c) -> c (o j)", c=128), writes=[glub], allow_slow_non_contiguous=True)
                rmask = K.sb("rmask_sb", [128, NS_TOK], F32)
                K.dma(SP, rmask[:], rmask_in[:, NPB:TB].partition_broadcast(128), writes=[rmask])
                tposb = K.sb("tposb", [128, TB], F32)
                rts = K.sb("rts", [128, 32, NS_TOK], F32)
                K.op(DVE, lambda: V.tensor_tensor(out=rts[:], in0=rmask[:, :].unsqueeze(1).to_broadcast([128, 32, NS_TOK]),
                                                  in1=mag[:].unsqueeze(2).to_broadcast([128, 32, NS_TOK]), op=ALU.mult),
                     reads=[rmask, mag], writes=[rts])
                halfpi = K.sb("halfpi", [128, 1], F32)
                K.op(DVE, lambda: V.memset(halfpi[:], math.pi / 2), writes=[halfpi])
                carry = K.sb("carry", [128, 2, 32], F32)
                K.op(DVE, lambda: V.memset(carry[:], 0.0), writes=[carry])
                xfin = K.sb("xfin", [128, 2, 32], F32)
                lx0 = K.sb("lx0", [128, 2, 32, NS], F32)
                xsf = K.sb("xsf", [128, 2, 32, NSB], F32)
                bdt_b = Buf(bdt_d, "bdt_d")
                ct_b = Buf(ct_d, "ct_d")

                if True:
                    class _Vw:
                        def __init__(self, buf, fn):
                            self.buf, self.fn = buf, fn

                        def __getitem__(self, k):
                            return self.fn(self.buf.t)[k]
                    v3 = lambda b_: _Vw(b_, lambda t: t[:, 0:512].rearrange("p (k c) -> p k c", c=16))
                    bpp = [v3(Fs[0]), v3(Fs[1])]
                    bbp = [v3(Fs[2]), v3(Fs[3])]
                    btmp = v3(Fs[4])
                    PP3 = "(k g2) p c -> (g2 p) k c"
                    K.dma(SP, bpp[0][:], bre_in.rearrange(PP3, g2=2), writes=[Fs[0]])
                    K.dma(SP, bpp[1][:], bim_in.rearrange(PP3, g2=2), writes=[Fs[1]])

                    def bc3(b_):
                        return b_[:].unsqueeze(2).to_broadcast([128, 32, 16])
                    rre, rim = sp["rre"], sp["rim"]
                    K.op(DVE, lambda: V.tensor_tensor(out=bbp[0][:], in0=bpp[0][:], in1=bc3(rre), op=ALU.mult), reads=[Fs[0], rre], writes=[Fs[2]])
                    K.op(DVE, lambda: V.tensor_tensor(out=btmp[:], in0=bpp[1][:], in1=bc3(rim), op=ALU.mult), reads=[Fs[1], rim], writes=[Fs[4]])
                    K.op(DVE, lambda: V.tensor_tensor(out=bbp[0][:], in0=bbp[0][:], in1=btmp[:], op=ALU.subtract), reads=[Fs[2], Fs[4]], writes=[Fs[2]])
                    K.op(DVE, lambda: V.tensor_tensor(out=bbp[1][:], in0=bpp[1][:], in1=bc3(rre), op=ALU.mult), reads=[Fs[1], rre], writes=[Fs[3]])
                    K.op(DVE, lambda: V.tensor_tensor(out=btmp[:], in0=bpp[0][:], in1=bc3(rim), op=ALU.mult), reads=[Fs[0], rim], writes=[Fs[4]])
                    K.op(DVE, lambda: V.tensor_tensor(out=bbp[1][:], in0=bbp[1][:], in1=btmp[:], op=ALU.add), reads=[Fs[3], Fs[4]], writes=[Fs[3]])
                    vc = lambda b_: _Vw(b_, lambda t: t[:, 0:512].rearrange("p (j q) -> p j q", q=64))
                    cnat = [vc(Fs[5]), vc(Fs[6])]
                    CN = "(j gp) c p -> (gp c) j p"
                    K.dma(SP, cnat[0][:], cre_in.rearrange(CN, gp=8), writes=[Fs[5]])
                    K.dma(SP, cnat[1][:], cim_in.rearrange(CN, gp=8), writes=[Fs[6]])
                    bbp_b = [Fs[2], Fs[3]]
                    cnat_b = [Fs[5], Fs[6]]
                    vz = lambda b_: _Vw(b_, lambda t: t[:, 0:512].rearrange("p (k c) -> p k c", c=128))
                    stg = [vz(Bs[2]), vz(Bs[3])]
                    zi = 0
                    for which in ("B", "C"):
                        for ri in range(2):
                            zb = xt[zi % 2]
                            zi += 1
                            z4 = zb.t[:].bitcast(BF16).rearrange("p (j k c) -> p j k c", k=4, c=128)
                            K.op(DVE, lambda: V.memset(zb.t[:].bitcast(BF16), 0.0), writes=[zb])
                            for kk in range(4):
                                for e in range(2):
                                    if which == "B":
                                        gq = 2 * kk + e
                                        src4 = bbp[ri][:].rearrange("p (j k) c -> p j k c", k=4)
                                        K.op(DVE, lambda: V.tensor_scalar(out=z4[:, :, kk, gq * 16:gq * 16 + 16],
                                                                          in0=src4[:, :, kk, :],
                                                                          scalar1=hm[:, e:e + 1], scalar2=None, op0=ALU.mult),
                                             reads=[bbp_b[ri], hm], writes=[zb])
                                    else:
                                        msk = gm if ri == 0 else gmn
                                        K.op(DVE, lambda: V.tensor_scalar(out=z4[:, :, kk, e * 64:e * 64 + 64],
                                                                          in0=cnat[ri][:],
                                                                          scalar1=msk[:, 2 * kk + e:2 * kk + e + 1],
                                                                          scalar2=None, op0=ALU.mult),
                                             reads=[cnat_b[ri], msk], writes=[zb])
                            for j in range(8):
                                st, stb = stg[j % 2], Bs[2 + j % 2]
                                for kk in range(4):
                                    K.op(PE, lambda: T.transpose(out=tp[:, kk * 128:(kk + 1) * 128], in_=z4[:, j, kk, :],
                                                                 identity=ident[:, :]),
                                         reads=[zb, ident], writes=[tp], inc=(kk == 3))
                                K.op(ACT, lambda: A.copy(out=st[:], in_=tp[:, 0:512].rearrange("p (k c) -> p k c", c=128)),
                                     reads=[tp], writes=[stb])
                                dst, dbuf = (bdt_d, bdt_b) if which == "B" else (ct_d, ct_b)
                                K.dma(SP, dst[4 * j:4 * j + 4, :, ri, :].rearrange("k p c -> p k c"), st[:],
                                      reads=[stb], writes=[dbuf])
                    vx = lambda b_: _Vw(b_, lambda t: t[:, 0:32 * NS].rearrange("p (k s) -> p k s", s=NS))
                    x0p = [vx(oTh[0]), vx(oTh[1])]
                    x0p_b = [oTh[0], oTh[1]]
                    srcs = [sre.rearrange("s g p -> s (g p)"), sim_.rearrange("s g p -> s (g p)")]
                    for ri in range(2):
                        for hf in range(2):
                            xn = xt[2 * ri + hf]
                            K.dma(SP, xn[:NS, :], srcs[ri][:, hf * 2048:(hf + 1) * 2048], writes=[xn])
                            for k in range(16):
                                K.op(PE, lambda: T.transpose(out=pA[:, k * NS:(k + 1) * NS], in_=xn[:NS, k * 128:(k + 1) * 128],
                                                             identity=identf[:NS, :NS]),
                                     reads=[xn, identf], writes=[pA], inc=(k == 15))
                            K.op(ACT, lambda: A.copy(out=x0p[ri][:, hf * 16:(hf + 1) * 16, :],
                                                     in_=pA[:, 0:16 * NS].rearrange("p (k s) -> p k s", s=NS)),
                                 reads=[pA], writes=[x0p_b[ri]])
                    xtmp = vx(oTh[2])

                    def bcs(b_):
                        return b_[:].unsqueeze(2).to_broadcast([128, 32, NS])
                    K.op(DVE, lambda: V.tensor_tensor(out=lx0[:, 0], in0=x0p[0][:], in1=bcs(lre), op=ALU.mult), reads=[oTh[0], lre], writes=[lx0])
                    K.op(DVE, lambda: V.tensor_tensor(out=xtmp[:], in0=x0p[1][:], in1=bcs(lim), op=ALU.mult), reads=[oTh[1], lim], writes=[oTh[2]])
                    K.op(DVE, lambda: V.tensor_tensor(out=lx0[:, 0], in0=lx0[:, 0], in1=xtmp[:], op=ALU.subtract), reads=[lx0, oTh[2]], writes=[lx0])
                    K.op(DVE, lambda: V.tensor_tensor(out=lx0[:, 1], in0=x0p[1][:], in1=bcs(lre), op=ALU.mult), reads=[oTh[1], lre], writes=[lx0])
                    K.op(DVE, lambda: V.tensor_tensor(out=xtmp[:], in0=x0p[0][:], in1=bcs(lim), op=ALU.mult), reads=[oTh[0], lim], writes=[oTh[2]])
                    K.op(DVE, lambda: V.tensor_tensor(out=lx0[:, 1], in0=lx0[:, 1], in1=xtmp[:], op=ALU.add), reads=[lx0, oTh[2]], writes=[lx0])

                wbd = [K.sb(f"wbd{i}", [128, 2, 128], BF16) for i in range(2)]
                wct = [K.sb(f"wct{i}", [128, 2, 128], BF16) for i in range(2)]

                def scol(b_, pos):
                    return b_[:, NPB:TB].rearrange("p (s t) -> p s t", t=4)[:, :, pos]

                fin = K.sb("s5fin", [128, 2, NS_TOK + 1], F32)
                _fm = K.sb("s5fm0", [128, NS_TOK + 1], F32)
                fm = [_fm, _fm]

                s5_state = {}

                def s5_uproj():
                    ub = ybj
                    for jj in range(4):
                        slot = next_wu()
                        K.dma(POOL, slot[:], w_in_v[:, :, 4096 + jj * 256:4096 + (jj + 1) * 256], writes=[slot])
                        for hf in range(2):
                            j = 2 * jj + hf
                            a = next_acc()
                            project(a, slot, c0=hf * 128)
                            K.op(ACT, lambda: A.copy(out=ub[j][:], in_=a[:, 0:TB]), reads=[a], writes=[ub[j]])
                            yield

                def s5_block(blk, full=True, trow=0):
                    K.dma(SP, tposb[:], tpos_in[trow:trow + 1, :].partition_broadcast(128), writes=[tposb])
                    ub = ybj
                    if not s5_state.pop("uproj_done", False):
                        run_streams([s5_uproj()])
                    glu_pref = []
                    if full:
                        for jj in range(2):
                            sl_ = next_wu()
                            K.dma(POOL, sl_[:, 0:8, :], gluw_v[:, :, jj * 256:(jj + 1) * 256], writes=[sl_])
                            glu_pref.append(sl_)
                        load_cb(0, 1)
                        opref["cb0"] = True
                    tabs = [(Fs[0], Fs[1], Fs[2]), (Fs[3], Fs[4], Fs[5])]
                    uins = [(Fs[6], Fs[7]), (Fs[8], oTh[0])]
                    tr, tn, m1a = oTh[1], oTh[2], oTh[3]
                    wr, wi = oTh[4], oTh[5]
                    yv, tg = oTh[6], oTh[7]
                    xb = [tuple(Bs[0:4]), tuple(Bs[0:4])]
                    csb, snb, nsnb, wrb, wib = Bs[4], Bs[5], Bs[6], Bs[7], Bs[8]
                    s0_, s1_ = blk * NSB, (blk + 1) * NSB
                    st = {}
                    c0 = NPB - 1

                    def stage_t(k):
                        cs, sn, nsn = tabs[k % 2]
                        K.op(ACT, lambda: A.activation(out=tr[:], in_=tposb[:], func=AF.Copy, scale=phif[:, k:k + 1]),
                             reads=[tposb, phif], writes=[tr])
                        K.op(DVE, lambda: V.tensor_scalar(out=tn[:], in0=tr[:], scalar1=MAGIC, scalar2=MAGIC,
                                                          op0=ALU.add, op1=ALU.subtract), reads=[tr], writes=[tn])
                        K.op(DVE, lambda: V.scalar_tensor_tensor(out=tr[:], in0=tn[:], scalar=-1.0, in1=tr[:],
                                                                 op0=ALU.mult, op1=ALU.add), reads=[tr, tn], writes=[tr])
                        K.op(ACT, lambda: A.activation(out=sn[:], in_=tr[:], func=AF.Sin, scale=TWO_PI), reads=[tr], writes=[sn])
                        K.op(ACT, lambda: A.activation(out=nsn[:], in_=tr[:], func=AF.Sin, scale=-TWO_PI), reads=[tr], writes=[nsn])
                        K.op(ACT, lambda: A.activation(out=tn[:], in_=tr[:], func=AF.Abs), reads=[tr], writes=[tn])
                        K.op(ACT, lambda: A.activation(out=cs[:], in_=tn[:], func=AF.Sin, scale=-TWO_PI, bias=halfpi[:, 0:1]),
                             reads=[tn, halfpi], writes=[cs])

                    def stage_a(k):
                        j = k // 4
                        cs, sn, nsn = tabs[k % 2]
                        ur, ui = uins[k % 2]
                        bd = wbd[k % 2]
                        ctw = wct[k % 2]
                        st[k] = (bd, ctw)
                        K.dma(SP, bd[:], bdt_d[k], reads=[bdt_b], writes=[bd])
                        K.dma(SP, ctw[:], ct_d[k], reads=[ct_b], writes=[ctw])
                        for ri in range(2):
                            a = acc[ri]
                            K.op(PE, lambda: T.matmul(a[:, 0:NPB], lhsT=bd[:, ri, :], rhs=ub[j][:, 0:NPB], start=True, stop=True),
                                 reads=[bd, ub[j]], writes=[a], inc=False)
                            K.op(PE, lambda: T.matmul(a[:, NPB:TB], lhsT=bd[:, ri, :], rhs=ub[j][:, NPB:TB], start=True, stop=True),
                                 reads=[bd, ub[j]], writes=[a])
                        K.op(DVE, lambda: V.tensor_tensor(out=ur[:], in0=acc[0][:, 0:TB], in1=cs[:], op=ALU.mult), reads=[acc[0], cs], writes=[ur])
                        K.op(DVE, lambda: V.tensor_tensor(out=m1a[:], in0=acc[1][:, 0:TB], in1=sn[:], op=ALU.mult), reads=[acc[1], sn], writes=[m1a])
                        K.op(POOL, lambda: G.tensor_tensor(out=ur[:], in0=ur[:], in1=m1a[:], op=ALU.add), reads=[ur, m1a], writes=[ur])
                        K.op(DVE, lambda: V.tensor_tensor(out=ui[:], in0=acc[1][:, 0:TB], in1=cs[:], op=ALU.mult), reads=[acc[1], cs], writes=[ui])
                        K.op(DVE, lambda: V.tensor_tensor(out=tn[:], in0=acc[0][:, 0:TB], in1=nsn[:], op=ALU.mult), reads=[acc[0], nsn], writes=[tn])
                        K.op(POOL, lambda: G.tensor_tensor(out=ui[:], in0=ui[:], in1=tn[:], op=ALU.add), reads=[ui, tn], writes=[ui])
                        if full:
                            K.op(DVE, lambda: V.tensor_tensor(out=scol(ur, 0), in0=scol(ur, 0), in1=lx0[:, 0, k, s0_:s1_], op=ALU.add),
                                 reads=[ur, lx0], writes=[ur])
                            K.op(DVE, lambda: V.tensor_tensor(out=scol(ui, 0), in0=scol(ui, 0), in1=lx0[:, 1, k, s0_:s1_], op=ALU.add),
                                 reads=[ui, lx0], writes=[ui])

                    def stage_s(k):
                        ur, ui = uins[k % 2]
                        magb = mag[:, k:k + 1].to_broadcast([128, NPB])
                        for (w_, u_, ri_) in ((wr, ur, 0), (wi, ui, 1)):
                            K.op(DVE, lambda: V.tensor_tensor_scan(out=w_[:, 0:NPB], data0=magb, data1=u_[:, 0:NPB],
                                                                   initial=carry[:, ri_, k:k + 1], op0=ALU.mult, op1=ALU.add),
                                 reads=[mag, u_, carry], writes=[w_])
                            if full:
                                K.op(DVE, lambda: V.tensor_tensor_scan(out=w_[:, NPB:TB], data0=rts[:, k, :], data1=u_[:, NPB:TB],
                                                                       initial=0.0, op0=ALU.mult, op1=ALU.add),
                                     reads=[rts, u_], writes=[w_])
                        K.op(ACT, lambda: A.copy(out=carry[:, 0, k:k + 1], in_=wr[:, NPB - 1:NPB]), reads=[wr], writes=[carry])
                        K.op(ACT, lambda: A.copy(out=carry[:, 1, k:k + 1], in_=wi[:, NPB - 1:NPB]), reads=[wi], writes=[carry])

                    def stage_r(k):
                        j = k // 4
                        kk = k % 4
                        cs, sn, nsn = tabs[k % 2]
                        p1, p2, p3, p4 = xb[k % 2]
                        bd, ctw = st.pop(k)
                        for (src_, dst_) in ((wr, wrb), (wi, wib), (cs, csb), (sn, snb), (nsn, nsnb)):
                            K.op(ACT, lambda: A.copy(out=dst_[:], in_=src_[:]), reads=[src_], writes=[dst_])
                        K.op(DVE, lambda: V.tensor_tensor(out=p1[:], in0=wrb[:], in1=csb[:], op=ALU.mult), reads=[wrb, csb], writes=[p1])
                        K.op(DVE, lambda: V.tensor_tensor(out=p2[:], in0=wib[:], in1=nsnb[:], op=ALU.mult), reads=[wib, nsnb], writes=[p2])
                        K.op(DVE, lambda: V.tensor_tensor(out=p3[:], in0=wib[:], in1=csb[:], op=ALU.mult), reads=[wib, csb], writes=[p3])
                        K.op(DVE, lambda: V.tensor_tensor(out=p4[:], in0=wrb[:], in1=snb[:], op=ALU.mult), reads=[wrb, snb], writes=[p4])
                        K.op(DVE, lambda: V.tensor_tensor(out=fin[:, 0, :], in0=wr[:, c0:TB], in1=cs[:, c0:TB], op=ALU.mult), reads=[wr, cs], writes=[fin])
                        K.op(DVE, lambda: V.tensor_tensor(out=fm[0][:], in0=wi[:, c0:TB], in1=nsn[:, c0:TB], op=ALU.mult), reads=[wi, nsn], writes=[fm[0]])
                        K.op(DVE, lambda: V.tensor_tensor(out=fin[:, 0, :], in0=fin[:, 0, :], in1=fm[0][:], op=ALU.add), reads=[fin, fm[0]], writes=[fin])
                        K.op(DVE, lambda: V.tensor_tensor(out=fin[:, 1, :], in0=wi[:, c0:TB], in1=cs[:, c0:TB], op=ALU.mult), reads=[wi, cs], writes=[fin])
                        K.op(DVE, lambda: V.tensor_tensor(out=fm[1][:], in0=wr[:, c0:TB], in1=sn[:, c0:TB], op=ALU.mult), reads=[wr, sn], writes=[fm[1]])
                        K.op(DVE, lambda: V.tensor_tensor(out=fin[:, 1, :], in0=fin[:, 1, :], in1=fm[1][:], op=ALU.add), reads=[fin, fm[1]], writes=[fin])
                        K.op(ACT, lambda: A.copy(out=xsf[:, :, k, :],
                                                 in_=fin[:, :, 1:1 + NS_TOK].rearrange("p r (s t) -> p r s t", t=4)[:, :, :, 3]),
                             reads=[fin], writes=[xsf])
                        if blk == nblk - 1:
                            K.op(ACT, lambda: A.copy(out=xfin[:, :, k:k + 1], in_=fin[:, :, 0:1]), reads=[fin], writes=[xfin])
                        prods = ((p1, 0), (p2, 0), (p3, 1), (p4, 1))
                        for (lo, hi, pbuf, pap) in ((0, NPB, pA, pA[:, 0:NPB]), (NPB, TB, pB, pB[:, 0:NS_TOK])):
                            for qi, (pp, ci) in enumerate(prods):
                                K.op(PE, lambda: T.matmul(pap, lhsT=ctw[:, ci, :], rhs=pp[:, lo:hi],
                                                          start=(kk == 0 and qi == 0), stop=(kk == 3 and qi == 3)),
                                     reads=[ctw, pp], writes=[pbuf], inc=(hi == TB and qi == 3))
                        if kk == 3:
                            K.op(DVE, lambda: V.scalar_tensor_tensor(out=yv[:, 0:NPB], in0=ub[j][:, 0:NPB], scalar=dcol[:, j:j + 1],
                                                                     in1=pA[:, 0:NPB], op0=ALU.mult, op1=ALU.add),
                                 reads=[ub[j], dcol, pA], writes=[yv])
                            K.op(DVE, lambda: V.scalar_tensor_tensor(out=yv[:, NPB:TB], in0=ub[j][:, NPB:TB], scalar=dcol[:, j:j + 1],
                                                                     in1=pB[:, 0:NS_TOK], op0=ALU.mult, op1=ALU.add),
                                 reads=[ub[j], dcol, pB], writes=[yv])
                            K.op(ACT, lambda: A.activation(out=tg[:], in_=yv[:], func=AF.Square), reads=[yv], writes=[tg])
                            K.op(POOL, lambda: G.tensor_scalar(out=tg[:], in0=tg[:], scalar1=0.044715, scalar2=1.0,
                                                               op0=ALU.mult, op1=ALU.add), reads=[tg], writes=[tg])
                            K.op(POOL, lambda: G.tensor_tensor(out=tg[:], in0=tg[:], in1=yv[:], op=ALU.mult), reads=[tg, yv], writes=[tg])
                            K.op(ACT, lambda: A.activation(out=tg[:], in_=tg[:], func=AF.Sigmoid, scale=2.0 * math.sqrt(2.0 / math.pi)),
                                 reads=[tg], writes=[tg])
                            K.op(DVE, lambda: V.tensor_tensor(out=ybj[j][:], in0=yv[:], in1=tg[:], op=ALU.mult), reads=[yv, tg], writes=[ybj[j]])

                    npair = dbg_npair
                    stage_t(0)
                    if npair > 1:
                        stage_t(1)
                    stage_a(0)
                    for k in range(npair):
                        stage_s(k)
                        if k + 1 < npair:
                            stage_a(k + 1)
                        if full:
                            stage_r(k)
                        else:
                            st.pop(k)
                        if k + 2 < npair:
                            stage_t(k + 2)
                    if not full:
                        return
                    for ri in range(2):
                        dst_ = (rs_o if ri == 0 else is_o)[blk * NSB:(blk + 1) * NSB].rearrange("s g p -> s (g p)")
                        for q in range(8):
                            stg_ = oTh[6 + q % 2]
                            for kk in range(4):
                                K.op(PE, lambda: T.transpose(out=pC[:NSB, kk * 128:(kk + 1) * 128], in_=xsf[:, ri, 4 * q + kk, :],
                                                             identity=identf[:, :]),
                                     reads=[xsf, identf], writes=[pC], inc=(kk == 3))
                            K.op(ACT, lambda: A.copy(out=stg_[:NSB, 0:512], in_=pC[:NSB, 0:512]), reads=[pC], writes=[stg_])
                            K.dma(SP, dst_[:, q * 512:(q + 1) * 512], stg_[:NSB, 0:512], reads=[stg_])
                    tg2 = oTh[7]
                    for jo in range(8):
                        if jo % 2 == 0:
                            if jo // 2 < len(glu_pref):
                                slot = glu_pref[jo // 2]
                            else:
                                slot = next_wu()
                                K.dma(POOL, slot[:, 0:8, :], gluw_v[:, :, jo * 128:(jo + 2) * 128], writes=[slot])
                        a = next_acc()
                        project(a, slot, nk=8, rhs=lambda kc, lo, hi: (ybj[kc], ybj[kc][:, lo:hi]), c0=(jo % 2) * 128)
                        K.op(ACT, lambda: A.activation(out=tg2[:], in_=a[:, 0:TB], func=AF.Sigmoid, bias=glub[:, jo:jo + 1]),
                             reads=[a, glub], writes=[tg2])
                        K.op(DVE, lambda: V.tensor_tensor(out=mixS[:, jo, :], in0=ybj[jo][:], in1=tg2[:], op=ALU.mult),
                             reads=[ybj[jo], tg2], writes=[mixS])

            flag = K.sb("flag_sb", [128, 1], F32)
            K.dma(SP, flag[:], flag_in, writes=[flag])
            blocks = [("pre", i) for i in range(npre)] + [("main", i) for i in range(nblk)]
            ngrp = 0 if dbg_nomlp else DFF // (128 * HG)
            pref = {}

            def load_wu_group(g_):
                sl = []
                for half in range(HG // 2):
                    w_ = next_wu()
                    col_ = (g_ * HG + 2 * half) * 128
                    K.dma(POOL, w_[:], mlp_up_v[:, :, col_:col_ + 256], writes=[w_])
                    sl.append(w_)
                return sl
            for bi, (mode, blk) in enumerate(blocks):
                full = mode == "main"
                src = xp if full else xpre
                for i in range(4):
                    K.dma(SP, xt[i][:], src[blk * NPB + i * 128: blk * NPB + (i + 1) * 128, :], writes=[xt[i]])
                if full:
                    K.dma(SP, xt[4][:NS_TOK, :], xs[blk * NS_TOK:(blk + 1) * NS_TOK, :], writes=[xt[4]])
                elif blk == 0:
                    K.op(DVE, lambda: V.memset(xt[4][:], 0.0), writes=[xt[4]])

                if stage != "dense":
                    norm_to_hT("g1")
                    if stage == "full":
                        hgrn_all(blk, full, tail=s5_uproj())
                        s5_state["uproj_done"] = True
                    else:
                        hgrn_all(blk, full)
                    if full:
                        hgrn_finish()
                    if stage == "full":
                        s5_block(blk, full, bi)
                    if not full:
                        if blk == npre - 1:
                            K.op(DVE, lambda: V.tensor_scalar(out=Sst[:], in0=Sst[:], scalar1=flag[:, 0:1], scalar2=None, op0=ALU.mult),
                                 reads=[Sst, flag], writes=[Sst])
                            if stage == "full":
                                K.op(DVE, lambda: V.tensor_scalar(out=carry[:], in0=carry[:], scalar1=flag[:, 0:1], scalar2=None,
                                                                  op0=ALU.mult), reads=[carry, flag], writes=[carry])
                        continue
                    out_proj(hook=(lambda: pref.__setitem__("wu", load_wu_group(0))) if ngrp else None)
                    if blk == nblk - 1:
                        for h in range(8):
                            K.dma(SP, hp[h], Sst[:, h, :], reads=[Sst])

                norm_to_hT("g2")
                rl = [Fs[0], Fs[1]]
                if pref.get("wu") is None and ngrp:
                    pref["wu"] = load_wu_group(0)
                wu_next = pref.pop("wu", None)
                for g in range(ngrp):
                    wds = []
                    for hc in range(HG):
                        col = (g * HG + hc) * 128
                        wdsl = next_wd()
                        K.dma(POOL, wdsl[:], mlp_dn[col:col + 128, :], writes=[wdsl])
                        wds.append(wdsl)
                    wu_cur = wu_next
                    for hc in range(HG):
                        wus = wu_cur[hc // 2]
                        a = next_acc()
                        project(a, wus, c0=(hc % 2) * 128)
                        r = rl[hc % 2]
                        hb = ybj[(g % 2) * HG + hc]
                        K.op(ACT, lambda: A.activation(out=r[:, :], in_=a[:, 0:TB], func=AF.Relu), reads=[a], writes=[r])
                        K.op(DVE, lambda: V.tensor_tensor(out=hb[:, :], in0=r[:, :], in1=r[:, :], op=ALU.mult),
                             reads=[r], writes=[hb])
                    if g + 1 < ngrp:
                        wu_next = load_wu_group(g + 1)
                    for i in range(5):
                        n = tile_n[i]
                        o = tile_off[i]
                        for cb in range(4):
                            ad = next_acc()
                            for hc in range(HG):
                                hb = ybj[(g % 2) * HG + hc]
                                K.op(PE, lambda: T.matmul(ad[:n, 0:512], lhsT=hb[:, o:o + n],
                                                          rhs=wds[hc][:, cb * 512:(cb + 1) * 512],
                                                          start=(hc == 0), stop=(hc == HG - 1)),
                                     reads=[hb, wds[hc]], writes=[ad], inc=(hc == HG - 1))
                            K.op(DVE, lambda: V.tensor_tensor(out=xt[i][:n, cb * 512:(cb + 1) * 512],
                                                              in0=ad[:n, 0:512], in1=xt[i][:n, cb * 512:(cb + 1) * 512],
                                                              op=ALU.add),
                                 reads=[ad, xt[i]], writes=[xt[i]])
                load_gain("gf")
                rms_stats_all(mixS, mixS[:].rearrange("p j t -> p (j t)")[:, 0:D])
                for i in range(5):
                    n = tile_n[i]
                    K.op(DVE, lambda: V.scalar_tensor_tensor(out=xt[i][:n, :], in0=xt[i][:n, :],
                                                             scalar=rstd[:n, i:i + 1], in1=gb[:n, :],
                                                             op0=ALU.mult, op1=ALU.mult),
                         reads=[xt[i], rstd, gb], writes=[xt[i]])
                    if i < 4:
                        K.dma(SP, yp[blk * NPB + i * 128: blk * NPB + (i + 1) * 128, :], xt[i][:], reads=[xt[i]])
                    else:
                        K.dma(SP, ys[blk * NS_TOK:(blk + 1) * NS_TOK, :], xt[4][:NS_TOK, :], reads=[xt[4]])

            if stage == "full":
                for ri in range(2):
                    K.op(PE, lambda: T.transpose(out=pC[:32, ri * 128:(ri + 1) * 128], in_=xfin[:, ri, :], identity=identf[:, :]),
                         reads=[xfin, identf], writes=[pC], inc=(ri == 1))
                K.op(ACT, lambda: A.copy(out=oTh[6][:32, 0:256], in_=pC[:32, 0:256]), reads=[pC], writes=[oTh[6]])
                K.dma(SP, rp.rearrange("(k g2) p -> k (g2 p)", g2=2), oTh[6][:32, 0:128], reads=[oTh[6]])
                K.dma(SP, ip.rearrange("(k g2) p -> k (g2 p)", g2=2), oTh[6][:32, 128:256], reads=[oTh[6]])

            for s, v in K.out_tokens.values():
                SP.e.wait_ge(s, v)
            for e in (PE, ACT, DVE, POOL):
                if e.cnt:
                    SP.e.wait_ge(e.sem, e.cnt)
    return nc


def const_inputs(nblk=NBLK, npre=NPRE, half=0):
    c = {}
    c["ident"] = np.eye(128, dtype=np.float32)
    s_ = np.arange(64)
    c["maskP"] = (s_[:, None] <= s_[None, :]).astype(np.float32)
    t_ = np.arange(NS_TOK)
    c["maskS"] = ((t_[:, None] <= t_[None, :]) & (t_[:, None] // 4 == t_[None, :] // 4)).astype(np.float32)
    cm = np.ones((1, TB), np.float32)
    cm[0, 0:NPB:64] = 0.0
    cm[0, NPB:TB:4] = 0.0
    c["cmask"] = cm
    c["seqm"] = (t_[:, None] // 4 == np.arange(NSB)[None, :]).astype(np.float32)
    tp_ = np.zeros((nblk + npre, TB), np.float32)
    for b in range(npre):
        tp_[b, :NPB] = b * NPB + np.arange(NPB)
    for b in range(nblk):
        tp_[npre + b, :NPB] = (half * npre + b) * NPB + np.arange(NPB)
    tp_[:, NPB:] = np.arange(NS_TOK) % 4
    c["tpos"] = tp_
    rm = np.ones((1, TB), np.float32)
    rm[0, NPB:TB:4] = 0.0
    c["rmask"] = rm
    q = np.arange(128)
    c["hm"] = (q[:, None] // 64 == np.arange(2)[None, :]).astype(np.float32)
    c["gm"] = (q[:, None] // 16 == np.arange(8)[None, :]).astype(np.float32)
    c["flag"] = np.full((128, 1), float(half), np.float32)
    return c


def weight_inputs(inputs):
    f = lambda k: np.ascontiguousarray(np.asarray(inputs[k], dtype=np.float32))
    return dict(w_in=f("w_in")[0], w_out=f("w_out")[0], g1=f("norm1_g"), g2=f("norm2_g"),
                gf=f("final_norm_g")[None], mlp_up=f("mlp_up")[0], mlp_dn=f("mlp_down")[0],
                lbl=f("hgrn_lb_logits"), hgn=f("hgrn_norm_g"),
                a_re=f("s5_a_re")[0], a_im=f("s5_a_im")[0], lstep=f("s5_log_step"),
                b_re=f("s5_b_re")[0], b_im=f("s5_b_im")[0], c_re=f("s5_c_re")[0], c_im=f("s5_c_im")[0],
                s5d=f("s5_d"), glu_w=f("glu_w")[0], glu_b=f("glu_b"))


def kernel(**inputs):
    f = lambda k: np.ascontiguousarray(np.asarray(inputs[k], dtype=np.float32))
    xpr, xsm, sth, s5r, s5i = f("x_prompt"), f("x_sample"), f("state_hgrn"), f("state_s5_re"), f("state_s5_im")
    nc = build(nblk=NBLK, stage="full", npre=NPRE)
    wts = weight_inputs(inputs)
    HALF = NBLK * NPB
    in_maps = []
    for c in range(8):
        seq, half = c // 2, c % 2
        m = const_inputs(NBLK, NPRE, half)
        m.update(wts)
        m["xp"] = np.ascontiguousarray(xpr[seq, half * HALF:(half + 1) * HALF])
        m["xpre"] = np.ascontiguousarray(xpr[seq, 0:NPRE * NPB])
        m["xs"] = np.ascontiguousarray(xsm[16 * c:16 * c + 16].reshape(64, D))
        m["sh"] = np.ascontiguousarray(sth[0, 16 * c:16 * c + 16])
        m["sre"] = np.ascontiguousarray(s5r[0, 16 * c:16 * c + 16])
        m["sim"] = np.ascontiguousarray(s5i[0, 16 * c:16 * c + 16])
        in_maps.append(m)
    res = run_bass_kernel_spmd(nc, in_maps, core_ids=list(range(8))).results
    y_prompt = np.stack([np.concatenate([res[2 * q]["yp"], res[2 * q + 1]["yp"]]) for q in range(4)]).astype(np.float32)
    y_sample = np.concatenate([res[c]["ys"].reshape(16, 4, D) for c in range(8)]).astype(np.float32)
    hp = np.stack([res[2 * q + 1]["hp"] for q in range(4)])[None].astype(np.float32)
    rp = np.stack([res[2 * q + 1]["rp"] for q in range(4)])[None].astype(np.float32)
    ip = np.stack([res[2 * q + 1]["ip"] for q in range(4)])[None].astype(np.float32)
    hs = np.concatenate([res[c]["hs"] for c in range(8)])[None].astype(np.float32)
    rs = np.concatenate([res[c]["rs"] for c in range(8)])[None].astype(np.float32)
    is_ = np.concatenate([res[c]["is_"] for c in range(8)])[None].astype(np.float32)
    return (y_prompt, y_sample, hp, rp, ip, hs, rs, is_)
```

```python
from contextlib import ExitStack
import math
import os
import numpy as np
import concourse.bass as bass
import concourse.mybir as mybir
from concourse.bass_utils import run_bass_kernel_spmd

F32 = mybir.dt.float32
BF16 = mybir.dt.bfloat16
AF = mybir.ActivationFunctionType
ALU = mybir.AluOpType

D = 2048
KC = 16
NPB = 512
NSB = 8
NS_TOK = 4 * NSB
TB = NPB + NS_TOK
NBLK = 2
NPRE = 2
EPS = 1e-6
DFF = 8192
HG = 4
TWO_PI = 2.0 * math.pi
MAGIC = 12582912.0


class Buf:
    def __init__(self, t, name):
        self.t = t
        self.name = name
        self.w = None
        self.r = {}
        self.dsem = None
        self.dcnt = 0

    def __getitem__(self, k):
        return self.t[k]


class Eng:
    def __init__(self, e, sem, name):
        self.e = e
        self.sem = sem
        self.cnt = 0
        self.name = name
        self.waited = {}


class KB:
    def __init__(self, nc, es):
        self.nc = nc
        self.es = es
        self.nsem = 0
        self.PE = self.mk(nc.tensor, "pe")
        self.ACT = self.mk(nc.scalar, "act")
        self.DVE = self.mk(nc.vector, "dve")
        self.POOL = self.mk(nc.gpsimd, "pool")
        self.SP = self.mk(nc.sync, "sp")
        self.out_tokens = {}

    def newsem(self, name):
        self.nsem += 1
        return self.es.enter_context(self.nc.semaphore(name))

    def mk(self, e, name):
        return Eng(e, self.newsem("s_" + name), name)

    def sb(self, name, shape, dt=F32, es=None):
        t = (es or self.es).enter_context(self.nc.sbuf_tensor(name, shape, dt))
        return Buf(t, name)

    def ps(self, name, shape, dt=F32, es=None):
        t = (es or self.es).enter_context(self.nc.psum_tensor(name, shape, dt))
        return Buf(t, name)

    def _deps(self, eng, reads, writes):
        deps = {}

        def add(tok):
            if tok is None:
                return
            s, v = tok
            if deps.get(id(s), (None, 0))[1] < v:
                deps[id(s)] = (s, v)
        for b in reads:
            add(b.w)
        for b in writes:
            add(b.w)
            for tok in b.r.values():
                add(tok)
        for s, v in deps.values():
            if s is eng.sem and (eng.name == "pe" or v > eng.cnt):
                continue
            if eng.waited.get(id(s), 0) < v:
                eng.e.wait_ge(s, v)
                eng.waited[id(s)] = v

    @staticmethod
    def _mark(tok, reads, writes):
        for b in reads:
            if b.r.get(id(tok[0]), (None, 0))[1] < tok[1]:
                b.r[id(tok[0])] = tok
        for b in writes:
            b.w = tok
            b.r = {}

    def op(self, eng, fn, reads=(), writes=(), inc=True):
        self._deps(eng, reads, writes)
        ins = fn()
        if inc:
            eng.cnt += 1
            ins.then_inc(eng.sem, 1)
            tok = (eng.sem, eng.cnt)
        else:
            tok = (eng.sem, eng.cnt + 1)
        self._mark(tok, reads, writes)
        return ins

    def dma(self, eng, out_ap, in_ap, reads=(), writes=(), **kw):
        self._deps(eng, reads, writes)
        owner = writes[0] if writes else reads[0]
        if owner.dsem is None:
            owner.dsem = self.newsem("d_" + owner.name)
        owner.dcnt += 16
        eng.e.dma_start(out=out_ap, in_=in_ap, **kw).then_inc(owner.dsem, 16)
        tok = (owner.dsem, owner.dcnt)
        self._mark(tok, reads, writes)
        self.out_tokens[id(tok[0])] = tok
        return tok


def build(nblk=NBLK, stage="full", npre=NPRE):
    nc = bass.Bass("TRN2", target_bir_lowering=False)
    NP = nblk * NPB
    NS = nblk * NSB
    dbg_nomlp = os.environ.get("DBG_NOMLP") == "1"
    dbg_npair = int(os.environ.get("DBG_NPAIR", "32"))

    def din(name, shape):
        return nc.dram_tensor(name, list(shape), F32, kind="ExternalInput").ap()

    def dout(name, shape):
        return nc.dram_tensor(name, list(shape), F32, kind="ExternalOutput").ap()

    xp = din("xp", [NP, D])
    xpre = din("xpre", [max(npre, 1) * NPB, D])
    flag_in = din("flag", [128, 1])
    xs = din("xs", [NS * 4, D])
    w_in = din("w_in", [D, 5120])
    w_out = din("w_out", [D, D])
    g1 = din("g1", [1, D])
    g2 = din("g2", [1, D])
    gf = din("gf", [1, D])
    mlp_up = din("mlp_up", [D, DFF])
    mlp_dn = din("mlp_dn", [DFF, D])
    ident_in = din("ident", [128, 128])
    lbl_in = din("lbl", [2, 1024])
    hgn_in = din("hgn", [1, 1024])
    sh = din("sh", [NS, 8, 128, 128])
    maskP_in = din("maskP", [64, 64])
    maskS_in = din("maskS", [NS_TOK, NS_TOK])
    cmask_in = din("cmask", [1, TB])
    seqm_in = din("seqm", [NS_TOK, NSB])
    are_in = din("a_re", [64, 64])
    aim_in = din("a_im", [64, 64])
    lst_in = din("lstep", [1, 64])
    bre_in = din("b_re", [64, 64, 16])
    bim_in = din("b_im", [64, 64, 16])
    cre_in = din("c_re", [64, 16, 64])
    cim_in = din("c_im", [64, 16, 64])
    sd_in = din("s5d", [1, 1024])
    gluw = din("glu_w", [1024, 1024])
    glub_in = din("glu_b", [1, 1024])
    sre = din("sre", [NS, 64, 64])
    sim_ = din("sim", [NS, 64, 64])
    tpos_in = din("tpos", [nblk + npre, TB])
    rmask_in = din("rmask", [1, TB])
    hm_in = din("hm", [128, 2])
    gm_in = din("gm", [128, 8])

    yp = dout("yp", [NP, D])
    ys = dout("ys", [NS * 4, D])
    hp = dout("hp", [8, 128, 128])
    hs = dout("hs", [NS, 8, 128, 128])
    rp = dout("rp", [64, 64])
    ip = dout("ip", [64, 64])
    rs_o = dout("rs", [NS, 64, 64])
    is_o = dout("is_", [NS, 64, 64])
    bdt_d = nc.dram_tensor("bdt_d", [32, 128, 2, 128], BF16, kind="Internal").ap()
    ct_d = nc.dram_tensor("ct_d", [32, 128, 2, 128], BF16, kind="Internal").ap()

    with ExitStack() as es:
        K = KB(nc, es)
        PE, ACT, DVE, POOL, SP = K.PE, K.ACT, K.DVE, K.POOL, K.SP
        V = nc.vector
        A = nc.scalar
        G = nc.gpsimd
        T = nc.tensor

        ident = K.sb("ident_sb", [128, 128], BF16)
        K.dma(POOL, ident[:], ident_in, writes=[ident])
        identf = K.sb("identf_sb", [128, 128], F32)
        K.dma(SP, identf[:], ident_in, writes=[identf])
        gb = K.sb("gb", [128, D], F32)
        gsrc = {"g1": g1, "g2": g2, "gf": gf}

        def load_gain(nm):
            K.dma(SP, gb[:], gsrc[nm].partition_broadcast(128), writes=[gb])

        xt = [K.sb(f"xt{i}", [128, D], F32) for i in range(5)]
        tile_n = [128, 128, 128, 128, NS_TOK]
        tile_off = [0, 128, 256, 384, 512]
        hT = K.sb("hT", [128, KC, TB], BF16)
        htmh = [K.sb(f"htm{i}", [128, D // 2], BF16) for i in range(2)]
        ss = K.sb("ss_sb", [128, 8], F32)
        rs = K.sb("rs_sb", [128, 8], F32)
        rstd = K.sb("rstd", [128, 8], F32)
        K.op(DVE, lambda: V.memset(ss[:], 1.0), writes=[ss])
        Fs = [K.sb(f"F{i}", [128, TB], F32) for i in range(9)]
        Bs = [K.sb(f"B{i}", [128, TB], BF16) for i in range(9)]
        oTh = [K.sb(f"oT{i}", [128, TB], F32) for i in range(8)]
        gTh = [K.sb(f"gT{i}", [128, TB], BF16) for i in range(8)]
        ybj = [K.sb(f"yb{i}", [128, TB], BF16) for i in range(8)]
        mixS = K.sb("mixS", [128, 8, TB], BF16)
        NWU = 2
        wu = [K.sb(f"wu{i}", [128, KC, 256], BF16) for i in range(NWU)]
        NWD = 5
        wd = [K.sb(f"wd{i}", [128, D], BF16) for i in range(NWD)]

        def rms_stats_all(junk_buf, junk_ap):
            for i in range(5):
                n = tile_n[i]
                K.op(ACT, lambda: A.activation(out=junk_ap[:n, :], in_=xt[i][:n, :], func=AF.Square,
                                               accum_out=ss[:n, i:i + 1]),
                     reads=[xt[i]], writes=[junk_buf, ss])
            K.op(DVE, lambda: V.tensor_scalar(out=rs[:, 0:5], in0=ss[:, 0:5],
                                              scalar1=1.0 / D, scalar2=EPS, op0=ALU.mult, op1=ALU.add),
                 reads=[ss], writes=[rs])
            K.op(ACT, lambda: A.activation(out=rs[:, 0:5], in_=rs[:, 0:5], func=AF.Sqrt), reads=[rs], writes=[rs])
            K.op(DVE, lambda: V.reciprocal(out=rstd[:, 0:5], in_=rs[:, 0:5]), reads=[rs], writes=[rstd])

        with ExitStack() as pes:
            tp = K.ps("tp", [128, 1024], BF16, es=pes)
            acc = [K.ps(f"acc{i}", [128, 1024], F32, es=pes) for i in range(2)]
            pA = K.ps("pA", [128, 512], F32, es=pes)
            pB = K.ps("pB", [128, 512], F32, es=pes)
            pC = K.ps("pC", [128, 512], F32, es=pes)
            ctr = {"wu": 0, "wd": 0, "acc": 0, "kv": 0, "s0": 0, "pr": 0}

            def next_acc():
                a = acc[ctr["acc"] % 2]
                ctr["acc"] += 1
                return a

            def next_wu():
                s_ = wu[ctr["wu"] % NWU]
                ctr["wu"] += 1
                return s_

            def next_wd():
                s_ = wd[ctr["wd"] % NWD]
                ctr["wd"] += 1
                return s_

            def norm_to_hT(gname):
                load_gain(gname)
                rms_stats_all(mixS, mixS[:].rearrange("p j t -> p (j t)")[:, 0:D])
                for i in range(5):
                    n = tile_n[i]
                    for half in range(2):
                        hb_ = htmh[half]
                        K.op(DVE, lambda: V.scalar_tensor_tensor(out=hb_[:n, :], in0=xt[i][:n, half * 1024:(half + 1) * 1024],
                                                                 scalar=rstd[:n, i:i + 1], in1=gb[:n, half * 1024:(half + 1) * 1024],
                                                                 op0=ALU.mult, op1=ALU.mult),
                             reads=[xt[i], rstd, gb], writes=[hb_])
                        for kc in range(8):
                            K.op(PE, lambda: T.transpose(out=tp[:, kc * 128:kc * 128 + n],
                                                         in_=hb_[:n, kc * 128:(kc + 1) * 128],
                                                         identity=ident[:n, :n]),
                                 reads=[hb_, ident], writes=[tp], inc=(kc == 7))
                        K.op(ACT, lambda: A.copy(out=hT[:, half * 8:half * 8 + 8, tile_off[i]:tile_off[i] + n],
                                                 in_=tp[:].rearrange("p (k c) -> p k c", c=128)[:, :, :n]),
                             reads=[tp], writes=[hT])

            def project(a, slot, nk=KC, rhs=None, c0=0):
                def lhs(kc):
                    if isinstance(slot, tuple):
                        sb_ = slot[kc // 8]
                        return sb_, sb_[:].rearrange("p (k c) -> p k c", c=256)[:, kc % 8, c0:c0 + 128]
                    return slot, slot[:, kc, c0:c0 + 128]
                for (lo, hi, last) in ((0, NPB, False), (NPB, TB, True)):
                    for kc in range(nk):
                        r_buf, r_ap = (hT, hT[:, kc, lo:hi]) if rhs is None else rhs(kc, lo, hi)
                        l_buf, l_ap = lhs(kc)
                        K.op(PE, lambda: T.matmul(a[:, lo:hi], lhsT=l_ap, rhs=r_ap,
                                                  start=(kc == 0), stop=(kc == nk - 1)),
                             reads=[l_buf, r_buf], writes=[a], inc=(last and kc == nk - 1))

            w_in_v = w_in.rearrange("(k p) c -> p k c", p=128)
            mlp_up_v = mlp_up.rearrange("(k p) c -> p k c", p=128)
            gluw_v = gluw.rearrange("(k p) c -> p k c", p=128)

            def proj_pair(base, h, slot):
                if h % 2 == 0:
                    src_ = w_in_v[:, :, base + h * 128:base + h * 128 + 256]
                    if isinstance(slot, tuple):
                        for hf_ in range(2):
                            K.dma(POOL, slot[hf_][:].rearrange("p (k c) -> p k c", c=256), src_[:, hf_ * 8:(hf_ + 1) * 8, :],
                                  writes=[slot[hf_]])
                    else:
                        K.dma(POOL, slot[:], src_, writes=[slot])
                a = next_acc()
                project(a, slot, c0=(h % 2) * 128)
                return a

            def proj_cols(col):
                slot = next_wu()
                K.dma(POOL, slot[:, :, 0:128], w_in_v[:, :, col:col + 128], writes=[slot])
                a = next_acc()
                project(a, slot)
                return a

            maskP = K.sb("maskP_sb", [64, 64], F32)
            K.dma(SP, maskP[:], maskP_in, writes=[maskP])
            maskS = K.sb("maskS_sb", [NS_TOK, NS_TOK], F32)
            K.dma(SP, maskS[:], maskS_in, writes=[maskS])
            cmask = K.sb("cmask_sb", [128, TB], F32)
            K.dma(SP, cmask[:], cmask_in.partition_broadcast(128), writes=[cmask])
            seqm = K.sb("seqm_sb", [NS_TOK, NSB], F32)
            K.dma(SP, seqm[:], seqm_in, writes=[seqm])
            lbl = K.sb("lbl_sb", [128, 2, 8], F32)
            K.dma(SP, lbl[:], lbl_in.rearrange("r (h k) -> k r h", k=128), writes=[lbl], allow_slow_non_contiguous=True)
            hgn = K.sb("hgn_sb", [128, 8], F32)
            K.dma(SP, hgn[:], hgn_in.rearrange("o (h k) -> k (o h)", k=128), writes=[hgn], allow_slow_non_contiguous=True)
            lb = K.sb("lb", [128, 8], F32)
            oml = K.sb("oml", [128, 8], F32)
            K.op(DVE, lambda: V.tensor_tensor(out=lb[:], in0=lbl[:, 0, :], in1=lbl[:, 1, :], op=ALU.subtract),
                 reads=[lbl], writes=[lb])
            K.op(ACT, lambda: A.activation(out=lb[:], in_=lb[:], func=AF.Sigmoid), reads=[lb], writes=[lb])
            K.op(DVE, lambda: V.tensor_scalar(out=oml[:], in0=lb[:], scalar1=-1.0, scalar2=1.0,
                                              op0=ALU.mult, op1=ALU.add), reads=[lb], writes=[oml])
            ones_b = K.sb("ones_b", [128, 128], BF16)
            onecol = K.sb("onecol", [128, 1], F32)
            K.op(DVE, lambda: V.memset(onecol[:], 1.0), writes=[onecol])
            K.op(DVE, lambda: V.memset(ones_b[:], 1.0), writes=[ones_b])

            t1, qf, ff, kf, lf, bc, ex, kde, rsb = Fs
            hsets = [tuple(Bs[0:4]), tuple(Bs[4:8])]
            sqb = Bs[8]
            ebls = [K.sb(f"ebl{i}", [128, 8 + NSB], F32) for i in range(2)]
            Sst = K.sb("Sst", [128, 8, 128], F32)
            Sb = K.sb("Sb", [128, 2, 128], BF16)
            kv = [K.sb(f"kv{i}", [128, 256], BF16) for i in range(2)]
            _scm = K.sb("scm0", [64, 64], BF16)
            scm = [_scm, _scm]
            S0all = K.sb("S0all", [128, NSB, 128], F32)
            s0v = [Buf(S0all.t[:, i, :], f"s0v{i}") for i in range(NSB)]

            def load_states(blk, h):
                K.dma(SP, S0all[:], sh[blk * NSB:(blk + 1) * NSB, h].rearrange("s k v -> k s v"), writes=[S0all] + s0v)

            def store_states(blk, h):
                K.dma(SP, hs[blk * NSB:(blk + 1) * NSB, h].rearrange("s k v -> k s v"), S0all[:], reads=[S0all] + s0v)
            _s0b = K.sb("s0b0", [128, 128], BF16)
            s0b = [_s0b, _s0b]
            _kdm = K.sb("kdm0", [NS_TOK, 128], BF16)
            kdm = [_kdm, _kdm]
            scp, opp, op2, Spp = pA, pB, pC, pC
            K.op(DVE, lambda: V.memset(Sst[:], 0.0), writes=[Sst])
            K.op(DVE, lambda: V.memset(Sb[:], 0.0), writes=[Sb])

            def hgrn_front(blk, h, full=True):
              qt, kt, kdT, vT = hsets[h % 2]
              ebl = ebls[h % 2]
              if full:
                a = proj_pair(0, h, (wd[0], wd[1]))
                K.op(ACT, lambda: A.activation(out=qf[:], in_=a[:, 0:TB], func=AF.Silu), reads=[a], writes=[qf])
                yield
              if True:
                a = proj_pair(1024, h, wu[0])
                K.op(ACT, lambda: A.activation(out=t1[:], in_=a[:, 0:TB], func=AF.Sigmoid), reads=[a], writes=[t1])
                K.op(DVE, lambda: V.tensor_scalar(out=ff[:], in0=t1[:], scalar1=oml[:, h:h + 1], scalar2=lb[:, h:h + 1],
                                                  op0=ALU.mult, op1=ALU.add), reads=[t1, oml, lb], writes=[ff])
                K.op(DVE, lambda: V.tensor_scalar(out=kf[:], in0=ff[:], scalar1=-1.0, scalar2=1.0,
                                                  op0=ALU.mult, op1=ALU.add), reads=[ff], writes=[kf])
                K.op(ACT, lambda: A.activation(out=lf[:], in_=ff[:], func=AF.Ln), reads=[ff], writes=[lf])
                yield
                if full:
                    K.op(DVE, lambda: V.tensor_tensor_scan(out=bc[:], data0=cmask[:], data1=lf[:], initial=0.0,
                                                           op0=ALU.mult, op1=ALU.add), reads=[cmask, lf], writes=[bc])
                else:
                    K.op(DVE, lambda: V.tensor_tensor_scan(out=bc[:, 0:NPB], data0=onecol[:, 0:1].to_broadcast([128, NPB]),
                                                           data1=lf[:, 0:NPB], initial=0.0,
                                                           op0=ALU.mult, op1=ALU.add), reads=[onecol, lf], writes=[bc])
                if full:
                    K.op(ACT, lambda: A.activation(out=ex[:], in_=bc[:], func=AF.Exp), reads=[bc], writes=[ex])
                    K.op(DVE, lambda: V.tensor_tensor(out=qt[:], in0=qf[:], in1=ex[:], op=ALU.mult),
                         reads=[qf, ex], writes=[qt])
                    K.op(ACT, lambda: A.activation(out=ex[:], in_=bc[:], func=AF.Exp, scale=-1.0), reads=[bc], writes=[ex])
                    K.op(DVE, lambda: V.tensor_tensor(out=kt[:], in0=kf[:], in1=ex[:], op=ALU.mult),
                         reads=[kf, ex], writes=[kt])
                yield
                if not full:
                    K.op(ACT, lambda: A.activation(out=kde[:, 0:NPB], in_=bc[:, 0:NPB],
                                                   func=AF.Exp, bias=bc[:, NPB - 1:NPB], scale=-1.0),
                         reads=[bc], writes=[kde])
                if full:
                    bcp = bc[:, 0:NPB].rearrange("p (c t) -> p c t", t=64)
                    bcs_ = bc[:, NPB:TB].rearrange("p (c t) -> p c t", t=4)
                    K.op(DVE, lambda: V.tensor_tensor(out=kde[:, 0:NPB].rearrange("p (c t) -> p c t", t=64),
                                                      in0=bcp[:, :, 63:64].to_broadcast([128, 8, 64]), in1=bcp, op=ALU.subtract),
                         reads=[bc], writes=[kde])
                    K.op(DVE, lambda: V.tensor_tensor(out=kde[:, NPB:TB].rearrange("p (c t) -> p c t", t=4),
                                                      in0=bcs_[:, :, 3:4].to_broadcast([128, NSB, 4]), in1=bcs_, op=ALU.subtract),
                         reads=[bc], writes=[kde])
                    K.op(ACT, lambda: A.activation(out=kde[:], in_=kde[:], func=AF.Exp), reads=[kde], writes=[kde])
                wd_ = TB if full else NPB
                K.op(DVE, lambda: V.tensor_tensor(out=kdT[:, 0:wd_], in0=kf[:, 0:wd_], in1=kde[:, 0:wd_], op=ALU.mult),
                     reads=[kf, kde], writes=[kdT])
                if full:
                    K.op(ACT, lambda: A.activation(out=ebl[:, 0:8],
                                                   in_=bc[:, 0:NPB].rearrange("p (c t) -> p c t", t=64)[:, :, 63],
                                                   func=AF.Exp), reads=[bc], writes=[ebl])
                else:
                    K.op(ACT, lambda: A.activation(out=ebl[:, 0:1], in_=bc[:, NPB - 1:NPB], func=AF.Exp), reads=[bc], writes=[ebl])
                if full:
                    K.op(ACT, lambda: A.activation(out=ebl[:, 8:8 + NSB],
                                                   in_=bc[:, NPB:TB].rearrange("p (c t) -> p c t", t=4)[:, :, 3],
                                                   func=AF.Exp), reads=[bc], writes=[ebl])
                yield
                a = proj_pair(2048, h, wu[1])
                K.op(ACT, lambda: A.copy(out=vT[:], in_=a[:, 0:TB]), reads=[a], writes=[vT])
                yield
                if full:
                    a = proj_pair(3072, h, (wd[2], wd[3]))
                    K.op(ACT, lambda: A.activation(out=gTh[h][:], in_=a[:, 0:TB], func=AF.Silu), reads=[a], writes=[gTh[h]])
                yield

            def hgrn_chunks(blk, h, full=True):
                qt, kt, kdT, vT = hsets[h % 2]
                ebl = ebls[h % 2]
                if full:
                    K.op(ACT, lambda: A.copy(out=Sb[:, h % 2, :], in_=Sst[:, h, :]), reads=[Sst], writes=[Sb])
                oT = oTh[h]
                if not full:
                    for c in range(4):
                        kvb = kv[ctr["kv"] % 2]
                        ctr["kv"] += 1
                        o = 128 * c
                        K.op(PE, lambda: T.transpose(out=tp[:, 0:128], in_=kdT[:, o:o + 128], identity=ident[:, :]),
                             reads=[kdT, ident], writes=[tp], inc=False)
                        K.op(PE, lambda: T.transpose(out=tp[:, 128:256], in_=vT[:, o:o + 128], identity=ident[:, :]),
                             reads=[vT, ident], writes=[tp])
                        K.op(ACT, lambda: A.copy(out=kvb[:, :], in_=tp[:, 0:256]), reads=[tp], writes=[kvb])
                        K.op(PE, lambda: T.matmul(Spp[:, 128:256], lhsT=kvb[:, 0:128], rhs=kvb[:, 128:256],
                                                  start=(c == 0), stop=(c == 3)), reads=[kvb], writes=[Spp], inc=(c == 3))
                        yield
                    K.op(DVE, lambda: V.scalar_tensor_tensor(out=Sst[:, h, :], in0=Sst[:, h, :],
                                                             scalar=ebl[:, 0:1], in1=Spp[:, 128:256],
                                                             op0=ALU.mult, op1=ALU.add),
                         reads=[Sst, ebl, Spp], writes=[Sst])
                    yield
                    return
                for c in range(9 if full else 8):
                    if c:
                        yield
                    n = 64 if c < 8 else NS_TOK
                    o = 64 * c
                    kvb = kv[ctr["kv"] % 2]
                    sm = scm[ctr["kv"] % 2]
                    ctr["kv"] += 1
                    K.op(PE, lambda: T.transpose(out=tp[:n, 0:128], in_=kdT[:, o:o + n], identity=ident[:, :]),
                         reads=[kdT, ident], writes=[tp], inc=False)
                    K.op(PE, lambda: T.transpose(out=tp[:n, 128:256], in_=vT[:, o:o + n], identity=ident[:, :]),
                         reads=[vT, ident], writes=[tp])
                    K.op(ACT, lambda: A.copy(out=kvb[:n, :], in_=tp[:n, 0:256]), reads=[tp], writes=[kvb])
                    if full:
                        K.op(PE, lambda: T.matmul(scp[:n, :n], lhsT=kt[:, o:o + n], rhs=qt[:, o:o + n],
                                                  start=True, stop=True), reads=[kt, qt], writes=[scp])
                        mk = maskP if c < 8 else maskS
                        K.op(DVE, lambda: V.tensor_tensor(out=sm[:n, :n], in0=scp[:n, :n], in1=mk[:n, :n], op=ALU.mult),
                             reads=[scp, mk], writes=[sm])
                    if c < 8:
                        if full:
                            K.op(PE, lambda: T.matmul(opp[:, :n], lhsT=kvb[:n, 128:256], rhs=sm[:n, :n],
                                                      start=True, stop=False), reads=[kvb, sm], writes=[opp], inc=False)
                            K.op(PE, lambda: T.matmul(opp[:, :n], lhsT=Sb[:, h % 2, :], rhs=qt[:, o:o + n],
                                                      start=False, stop=True), reads=[Sb, qt], writes=[opp])
                            K.op(ACT, lambda: A.copy(out=oT[:, o:o + n], in_=opp[:, :n]), reads=[opp], writes=[oT])
                        K.op(PE, lambda: T.matmul(Spp[:, 128:256], lhsT=kvb[:n, 0:128], rhs=kvb[:n, 128:256],
                                                  start=True, stop=True), reads=[kvb], writes=[Spp])
                        K.op(DVE, lambda: V.scalar_tensor_tensor(out=Sst[:, h, :], in0=Sst[:, h, :],
                                                                 scalar=ebl[:, c:c + 1], in1=Spp[:, 128:256],
                                                                 op0=ALU.mult, op1=ALU.add),
                             reads=[Sst, ebl, Spp], writes=[Sst])
                        K.op(ACT, lambda: A.copy(out=Sb[:, h % 2, :], in_=Sst[:, h, :]), reads=[Sst], writes=[Sb])
                    else:
                        K.op(PE, lambda: T.matmul(opp[:, :n], lhsT=kvb[:n, 128:256], rhs=sm[:n, :n],
                                                  start=True, stop=True), reads=[kvb, sm], writes=[opp])
                        K.op(ACT, lambda: A.copy(out=oT[:, o:o + n], in_=opp[:, :n]), reads=[opp], writes=[oT])
                        for i in range(NSB):
                            sq = blk * NSB + i
                            j = ctr["s0"] % 2
                            ctr["s0"] += 1
                            K.op(ACT, lambda: A.copy(out=s0b[j][:], in_=s0v[i][:, :]), reads=[s0v[i]], writes=[s0b[j]])
                            K.op(PE, lambda: T.matmul(op2[:, 4 * i:4 * i + 4], lhsT=s0b[j][:, :],
                                                      rhs=qt[:, o + 4 * i:o + 4 * i + 4], start=True, stop=True),
                                 reads=[s0b[j], qt], writes=[op2])
                            K.op(DVE, lambda: V.tensor_scalar(out=kdm[j][:n, :], in0=kvb[:n, 0:128],
                                                              scalar1=seqm[:n, i:i + 1], scalar2=None, op0=ALU.mult),
                                 reads=[kvb, seqm], writes=[kdm[j]])
                            K.op(PE, lambda: T.matmul(Spp[:, 128:256], lhsT=kdm[j][:n, :], rhs=kvb[:n, 128:256],
                                                      start=True, stop=True), reads=[kdm[j], kvb], writes=[Spp])
                            K.op(DVE, lambda: V.scalar_tensor_tensor(out=s0v[i][:, :], in0=s0v[i][:, :],
                                                                     scalar=ebl[:, 8 + i:9 + i], in1=Spp[:, 128:256],
                                                                     op0=ALU.mult, op1=ALU.add),
                                 reads=[s0v[i], ebl, Spp], writes=[s0v[i]])
                        K.op(DVE, lambda: V.tensor_tensor(out=oT[:, o:o + n], in0=op2[:, :n], in1=oT[:, o:o + n],
                                                          op=ALU.add), reads=[op2, oT], writes=[oT])
                        store_states(blk, h)
                        if h < 7:
                            load_states(blk, h + 1)

                yield

            def run_streams(gens):
                gens = list(gens)
                while gens:
                    for g_ in list(gens):
                        try:
                            next(g_)
                        except StopIteration:
                            gens.remove(g_)

            def hgrn_all(blk, full):
                if full:
                    load_states(blk, 0)
                run_streams([hgrn_front(blk, 0, full)])
                for h in range(8):
                    gs = [hgrn_chunks(blk, h, full)]
                    if h < 7:
                        gs.append(hgrn_front(blk, h + 1, full))
                    run_streams(gs)

            def hgrn_finish():
                a = next_acc()
                for h in range(8):
                    K.op(ACT, lambda: A.activation(out=sqb[:], in_=oTh[h][:], func=AF.Square), reads=[oTh[h]], writes=[sqb])
                    K.op(PE, lambda: T.matmul(a[:, 0:NPB], lhsT=ones_b[:], rhs=sqb[:, 0:NPB], start=(h == 0), stop=(h == 7)),
                         reads=[ones_b, sqb], writes=[a], inc=False)
                    K.op(PE, lambda: T.matmul(a[:, NPB:TB], lhsT=ones_b[:], rhs=sqb[:, NPB:TB], start=(h == 0), stop=(h == 7)),
                         reads=[ones_b, sqb], writes=[a])
                K.op(DVE, lambda: V.tensor_scalar(out=rsb[:], in0=a[:, 0:TB], scalar1=1.0 / 1024, scalar2=EPS,
                                                  op0=ALU.mult, op1=ALU.add), reads=[a], writes=[rsb])
                K.op(ACT, lambda: A.activation(out=rsb[:], in_=rsb[:], func=AF.Sqrt), reads=[rsb], writes=[rsb])
                K.op(DVE, lambda: V.reciprocal(out=rsb[:], in_=rsb[:]), reads=[rsb], writes=[rsb])
                for h in range(8):
                    K.op(DVE, lambda: V.scalar_tensor_tensor(out=t1[:], in0=oTh[h][:], scalar=hgn[:, h:h + 1],
                                                             in1=rsb[:], op0=ALU.mult, op1=ALU.mult),
                         reads=[oTh[h], hgn, rsb], writes=[t1])
                    K.op(DVE, lambda: V.tensor_tensor(out=gTh[h][:], in0=t1[:], in1=gTh[h][:], op=ALU.mult),
                         reads=[t1, gTh[h]], writes=[gTh[h]])

            w_out_v = w_out.rearrange("(c p) n -> p c n", p=128)
            opref = {}

            def load_cb(cb, which):
                if which == 0:
                    for hf in range(2):
                        K.dma(POOL, wu[hf][:], w_out_v[:, :, cb * 512 + hf * 256: cb * 512 + (hf + 1) * 256], writes=[wu[hf]])
                else:
                    for q4 in range(4):
                        K.dma(POOL, wd[q4][:].rearrange("p (c n) -> p c n", n=512),
                              w_out_v[:, 4 * q4:4 * q4 + 4, cb * 512:(cb + 1) * 512], writes=[wd[q4]])

            def rhs_of(cb, which, c, lo, hi):
                if which == 0:
                    hf = lo // 256
                    return wu[hf], wu[hf][:, c, lo - hf * 256:hi - hf * 256]
                q4 = c // 4
                return wd[q4], wd[q4][:].rearrange("p (c n) -> p c n", n=512)[:, c % 4, lo:hi]

            def out_proj(hook=None):
                if not opref.pop("cb0", False):
                    load_cb(0, 1)
                for cb in range(4):
                    which = 1 - cb % 2
                    if cb + 1 < 4:
                        load_cb(cb + 1, 1 - which)
                    for i in range(5):
                        n = tile_n[i]
                        o = tile_off[i]
                        ad = next_acc()
                        for hf in range(2):
                            for c in range(16):
                                mb_, ma_ = (gTh[c], gTh[c][:, o:o + n]) if c < 8 else (mixS, mixS[:, c - 8, o:o + n])
                                wb_, wa_ = rhs_of(cb, which, c, hf * 256, (hf + 1) * 256)
                                K.op(PE, lambda: T.matmul(ad[:n, hf * 256:(hf + 1) * 256], lhsT=ma_, rhs=wa_,
                                                          start=(c == 0), stop=(c == 15)),
                                     reads=[mb_, wb_], writes=[ad], inc=(hf == 1 and c == 15))
                        K.op(DVE, lambda: V.tensor_tensor(out=xt[i][:n, cb * 512:(cb + 1) * 512],
                                                          in0=ad[:n, 0:512], in1=xt[i][:n, cb * 512:(cb + 1) * 512],
                                                          op=ALU.add),
                             reads=[ad, xt[i]], writes=[xt[i]])
                if hook is not None:
                    hook()

            if stage == "full":
                PP = "(k g2) p -> (g2 p) k"
                are = K.sb("are", [128, 32], F32)
                aim = K.sb("aim", [128, 32], F32)
                lsp = K.sb("lsp", [128, 32], F32)
                K.dma(SP, are[:], are_in.rearrange(PP, g2=2), writes=[are], allow_slow_non_contiguous=True)
                K.dma(SP, aim[:], aim_in.rearrange(PP, g2=2), writes=[aim], allow_slow_non_contiguous=True)
                lsv = lst_in.rearrange("o (k g2) -> o g2 k", g2=2)
                for g2_ in range(2):
                    K.dma(SP, lsp[64 * g2_:64 * g2_ + 64, :], lsv[:, g2_, :].partition_broadcast(64), writes=[lsp],
                          allow_slow_non_contiguous=True)
                sp = {}
                for nm in ("dt", "mag", "th", "phif", "tmp", "tmp2", "sn", "cs", "lre", "lim", "den", "rre", "rim", "nr"):
                    sp[nm] = K.sb("s5_" + nm, [128, 32], F32)

                def s5op(eng, fn, reads, writes):
                    K.op(eng, fn, reads=[sp[r] if isinstance(r, str) else r for r in reads],
                         writes=[sp[w] if isinstance(w, str) else w for w in writes])
                s5op(ACT, lambda: A.activation(out=sp["dt"][:], in_=lsp[:], func=AF.Exp), [lsp], ["dt"])
                s5op(DVE, lambda: V.tensor_tensor(out=sp["tmp"][:], in0=are[:], in1=sp["dt"][:], op=ALU.mult), [are, "dt"], ["tmp"])
                s5op(ACT, lambda: A.activation(out=sp["mag"][:], in_=sp["tmp"][:], func=AF.Exp), ["tmp"], ["mag"])
                s5op(DVE, lambda: V.tensor_tensor(out=sp["th"][:], in0=aim[:], in1=sp["dt"][:], op=ALU.mult), [aim, "dt"], ["th"])
                s5op(DVE, lambda: V.tensor_scalar(out=sp["tmp"][:], in0=sp["th"][:], scalar1=1.0 / TWO_PI, scalar2=None,
                                                  op0=ALU.mult), ["th"], ["tmp"])
                s5op(DVE, lambda: V.tensor_scalar(out=sp["tmp2"][:], in0=sp["tmp"][:], scalar1=MAGIC, scalar2=None,
                                                  op0=ALU.add), ["tmp"], ["tmp2"])
                s5op(DVE, lambda: V.tensor_scalar(out=sp["tmp2"][:], in0=sp["tmp2"][:], scalar1=MAGIC, scalar2=None,
                                                  op0=ALU.subtract), ["tmp2"], ["tmp2"])
                s5op(DVE, lambda: V.tensor_tensor(out=sp["phif"][:], in0=sp["tmp"][:], in1=sp["tmp2"][:], op=ALU.subtract),
                     ["tmp", "tmp2"], ["phif"])
                s5op(ACT, lambda: A.activation(out=sp["sn"][:], in_=sp["phif"][:], func=AF.Sin, scale=TWO_PI), ["phif"], ["sn"])
                s5op(ACT, lambda: A.activation(out=sp["tmp"][:], in_=sp["phif"][:], func=AF.Abs), ["phif"], ["tmp"])
                s5op(DVE, lambda: V.tensor_scalar(out=sp["tmp"][:], in0=sp["tmp"][:], scalar1=-TWO_PI, scalar2=math.pi / 2,
                                                  op0=ALU.mult, op1=ALU.add), ["tmp"], ["tmp"])
                s5op(ACT, lambda: A.activation(out=sp["cs"][:], in_=sp["tmp"][:], func=AF.Sin), ["tmp"], ["cs"])
                s5op(DVE, lambda: V.tensor_tensor(out=sp["lre"][:], in0=sp["mag"][:], in1=sp["cs"][:], op=ALU.mult), ["mag", "cs"], ["lre"])
                s5op(DVE, lambda: V.tensor_tensor(out=sp["lim"][:], in0=sp["mag"][:], in1=sp["sn"][:], op=ALU.mult), ["mag", "sn"], ["lim"])
                s5op(DVE, lambda: V.tensor_tensor(out=sp["den"][:], in0=are[:], in1=are[:], op=ALU.mult), [are], ["den"])
                s5op(DVE, lambda: V.tensor_tensor(out=sp["tmp"][:], in0=aim[:], in1=aim[:], op=ALU.mult), [aim], ["tmp"])
                s5op(DVE, lambda: V.tensor_tensor(out=sp["den"][:], in0=sp["den"][:], in1=sp["tmp"][:], op=ALU.add), ["den", "tmp"], ["den"])
                s5op(DVE, lambda: V.reciprocal(out=sp["den"][:], in_=sp["den"][:]), ["den"], ["den"])
                s5op(DVE, lambda: V.tensor_scalar(out=sp["nr"][:], in0=sp["lre"][:], scalar1=-1.0, scalar2=None, op0=ALU.add),
                     ["lre"], ["nr"])
                s5op(DVE, lambda: V.tensor_tensor(out=sp["tmp"][:], in0=sp["nr"][:], in1=are[:], op=ALU.mult), ["nr", are], ["tmp"])
                s5op(DVE, lambda: V.tensor_tensor(out=sp["tmp2"][:], in0=sp["lim"][:], in1=aim[:], op=ALU.mult), ["lim", aim], ["tmp2"])
                s5op(DVE, lambda: V.tensor_tensor(out=sp["tmp"][:], in0=sp["tmp"][:], in1=sp["tmp2"][:], op=ALU.add), ["tmp", "tmp2"], ["tmp"])
                s5op(DVE, lambda: V.tensor_tensor(out=sp["rre"][:], in0=sp["tmp"][:], in1=sp["den"][:], op=ALU.mult), ["tmp", "den"], ["rre"])
                s5op(DVE, lambda: V.tensor_tensor(out=sp["tmp"][:], in0=sp["lim"][:], in1=are[:], op=ALU.mult), ["lim", are], ["tmp"])
                s5op(DVE, lambda: V.tensor_tensor(out=sp["tmp2"][:], in0=sp["nr"][:], in1=aim[:], op=ALU.mult), ["nr", aim], ["tmp2"])
                s5op(DVE, lambda: V.tensor_tensor(out=sp["tmp"][:], in0=sp["tmp"][:], in1=sp["tmp2"][:], op=ALU.subtract), ["tmp", "tmp2"], ["tmp"])
                s5op(DVE, lambda: V.tensor_tensor(out=sp["rim"][:], in0=sp["tmp"][:], in1=sp["den"][:], op=ALU.mult), ["tmp", "den"], ["rim"])
                phif, mag, lre, lim = sp["phif"], sp["mag"], sp["lre"], sp["lim"]

                hm = K.sb("hm_sb", [128, 2], F32)
                K.dma(SP, hm[:], hm_in, writes=[hm])
                gm = K.sb("gm_sb", [128, 8], F32)
                K.dma(SP, gm[:], gm_in, writes=[gm])
                gmn = K.sb("gmn_sb", [128, 8], F32)
                K.op(DVE, lambda: V.tensor_scalar(out=gmn[:], in0=gm[:], scalar1=-1.0, scalar2=None, op0=ALU.mult),
                     reads=[gm], writes=[gmn])
                dcol = K.sb("dcol", [128, 8], F32)
                K.dma(SP, dcol[:], sd_in.rearrange("o (j c) -> c (o j)", c=128), writes=[dcol], allow_slow_non_contiguous=True)
                glub = K.sb("glub", [128, 8], F32)
                K.dma(SP, glub[:], glub_in.rearrange("o (j c) -> c (o j)", c=128), writes=[glub], allow_slow_non_contiguous=True)
                rmask = K.sb("rmask_sb", [128, NS_TOK], F32)
                K.dma(SP, rmask[:], rmask_in[:, NPB:TB].partition_broadcast(128), writes=[rmask])
                tposb = K.sb("tposb", [128, TB], F32)
                rts = K.sb("rts", [128, 32, NS_TOK], F32)
                K.op(DVE, lambda: V.tensor_tensor(out=rts[:], in0=rmask[:, :].unsqueeze(1).to_broadcast([128, 32, NS_TOK]),
                                                  in1=mag[:].unsqueeze(2).to_broadcast([128, 32, NS_TOK]), op=ALU.mult),
                     reads=[rmask, mag], writes=[rts])
                halfpi = K.sb("halfpi", [128, 1], F32)
                K.op(DVE, lambda: V.memset(halfpi[:], math.pi / 2), writes=[halfpi])
                carry = K.sb("carry", [128, 2, 32], F32)
                K.op(DVE, lambda: V.memset(carry[:], 0.0), writes=[carry])
                xfin = K.sb("xfin", [128, 2, 32], F32)
                lx0 = K.sb("lx0", [128, 2, 32, NS], F32)
                xsf = K.sb("xsf", [128, 2, 32, NSB], F32)
                bdt_b = Buf(bdt_d, "bdt_d")
                ct_b = Buf(ct_d, "ct_d")

                if True:
                    class _Vw:
                        def __init__(self, buf, fn):
                            self.buf, self.fn = buf, fn

                        def __getitem__(self, k):
                            return self.fn(self.buf.t)[k]
                    v3 = lambda b_: _Vw(b_, lambda t: t[:, 0:512].rearrange("p (k c) -> p k c", c=16))
                    bpp = [v3(Fs[0]), v3(Fs[1])]
                    bbp = [v3(Fs[2]), v3(Fs[3])]
                    btmp = v3(Fs[4])
                    PP3 = "(k g2) p c -> (g2 p) k c"
                    K.dma(SP, bpp[0][:], bre_in.rearrange(PP3, g2=2), writes=[Fs[0]])
                    K.dma(SP, bpp[1][:], bim_in.rearrange(PP3, g2=2), writes=[Fs[1]])

                    def bc3(b_):
                        return b_[:].unsqueeze(2).to_broadcast([128, 32, 16])
                    rre, rim = sp["rre"], sp["rim"]
                    K.op(DVE, lambda: V.tensor_tensor(out=bbp[0][:], in0=bpp[0][:], in1=bc3(rre), op=ALU.mult), reads=[Fs[0], rre], writes=[Fs[2]])
                    K.op(DVE, lambda: V.tensor_tensor(out=btmp[:], in0=bpp[1][:], in1=bc3(rim), op=ALU.mult), reads=[Fs[1], rim], writes=[Fs[4]])
                    K.op(DVE, lambda: V.tensor_tensor(out=bbp[0][:], in0=bbp[0][:], in1=btmp[:], op=ALU.subtract), reads=[Fs[2], Fs[4]], writes=[Fs[2]])
                    K.op(DVE, lambda: V.tensor_tensor(out=bbp[1][:], in0=bpp[1][:], in1=bc3(rre), op=ALU.mult), reads=[Fs[1], rre], writes=[Fs[3]])
                    K.op(DVE, lambda: V.tensor_tensor(out=btmp[:], in0=bpp[0][:], in1=bc3(rim), op=ALU.mult), reads=[Fs[0], rim], writes=[Fs[4]])
                    K.op(DVE, lambda: V.tensor_tensor(out=bbp[1][:], in0=bbp[1][:], in1=btmp[:], op=ALU.add), reads=[Fs[3], Fs[4]], writes=[Fs[3]])
                    vc = lambda b_: _Vw(b_, lambda t: t[:, 0:512].rearrange("p (j q) -> p j q", q=64))
                    cnat = [vc(Fs[5]), vc(Fs[6])]
                    CN = "(j gp) c p -> (gp c) j p"
                    K.dma(SP, cnat[0][:], cre_in.rearrange(CN, gp=8), writes=[Fs[5]])
                    K.dma(SP, cnat[1][:], cim_in.rearrange(CN, gp=8), writes=[Fs[6]])
                    bbp_b = [Fs[2], Fs[3]]
                    cnat_b = [Fs[5], Fs[6]]
                    vz = lambda b_: _Vw(b_, lambda t: t[:, 0:512].rearrange("p (k c) -> p k c", c=128))
                    stg = [vz(Bs[2]), vz(Bs[3])]
                    zi = 0
                    for which in ("B", "C"):
                        for ri in range(2):
                            zb = xt[zi % 2]
                            zi += 1
                            z4 = zb.t[:].bitcast(BF16).rearrange("p (j k c) -> p j k c", k=4, c=128)
                            K.op(DVE, lambda: V.memset(zb.t[:].bitcast(BF16), 0.0), writes=[zb])
                            for kk in range(4):
                                for e in range(2):
                                    if which == "B":
                                        gq = 2 * kk + e
                                        src4 = bbp[ri][:].rearrange("p (j k) c -> p j k c", k=4)
                                        K.op(DVE, lambda: V.tensor_scalar(out=z4[:, :, kk, gq * 16:gq * 16 + 16],
                                                                          in0=src4[:, :, kk, :],
                                                                          scalar1=hm[:, e:e + 1], scalar2=None, op0=ALU.mult),
                                             reads=[bbp_b[ri], hm], writes=[zb])
                                    else:
                                        msk = gm if ri == 0 else gmn
                                        K.op(DVE, lambda: V.tensor_scalar(out=z4[:, :, kk, e * 64:e * 64 + 64],
                                                                          in0=cnat[ri][:],
                                                                          scalar1=msk[:, 2 * kk + e:2 * kk + e + 1],
                                                                          scalar2=None, op0=ALU.mult),
                                             reads=[cnat_b[ri], msk], writes=[zb])
                            for j in range(8):
                                st, stb = stg[j % 2], Bs[2 + j % 2]
                                for kk in range(4):
                                    K.op(PE, lambda: T.transpose(out=tp[:, kk * 128:(kk + 1) * 128], in_=z4[:, j, kk, :],
                                                                 identity=ident[:, :]),
                                         reads=[zb, ident], writes=[tp], inc=(kk == 3))
                                K.op(ACT, lambda: A.copy(out=st[:], in_=tp[:, 0:512].rearrange("p (k c) -> p k c", c=128)),
                                     reads=[tp], writes=[stb])
                                dst, dbuf = (bdt_d, bdt_b) if which == "B" else (ct_d, ct_b)
                                K.dma(SP, dst[4 * j:4 * j + 4, :, ri, :].rearrange("k p c -> p k c"), st[:],
                                      reads=[stb], writes=[dbuf])
                    vx = lambda b_: _Vw(b_, lambda t: t[:, 0:32 * NS].rearrange("p (k s) -> p k s", s=NS))
                    x0p = [vx(oTh[0]), vx(oTh[1])]
                    x0p_b = [oTh[0], oTh[1]]
                    srcs = [sre.rearrange("s g p -> s (g p)"), sim_.rearrange("s g p -> s (g p)")]
                    for ri in range(2):
                        for hf in range(2):
                            xn = xt[2 * ri + hf]
                            K.dma(SP, xn[:NS, :], srcs[ri][:, hf * 2048:(hf + 1) * 2048], writes=[xn])
                            for k in range(16):
                                K.op(PE, lambda: T.transpose(out=pA[:, k * NS:(k + 1) * NS], in_=xn[:NS, k * 128:(k + 1) * 128],
                                                             identity=identf[:NS, :NS]),
                                     reads=[xn, identf], writes=[pA], inc=(k == 15))
                            K.op(ACT, lambda: A.copy(out=x0p[ri][:, hf * 16:(hf + 1) * 16, :],
                                                     in_=pA[:, 0:16 * NS].rearrange("p (k s) -> p k s", s=NS)),
                                 reads=[pA], writes=[x0p_b[ri]])
                    xtmp = vx(oTh[2])

                    def bcs(b_):
                        return b_[:].unsqueeze(2).to_broadcast([128, 32, NS])
                    K.op(DVE, lambda: V.tensor_tensor(out=lx0[:, 0], in0=x0p[0][:], in1=bcs(lre), op=ALU.mult), reads=[oTh[0], lre], writes=[lx0])
                    K.op(DVE, lambda: V.tensor_tensor(out=xtmp[:], in0=x0p[1][:], in1=bcs(lim), op=ALU.mult), reads=[oTh[1], lim], writes=[oTh[2]])
                    K.op(DVE, lambda: V.tensor_tensor(out=lx0[:, 0], in0=lx0[:, 0], in1=xtmp[:], op=ALU.subtract), reads=[lx0, oTh[2]], writes=[lx0])
                    K.op(DVE, lambda: V.tensor_tensor(out=lx0[:, 1], in0=x0p[1][:], in1=bcs(lre), op=ALU.mult), reads=[oTh[1], lre], writes=[lx0])
                    K.op(DVE, lambda: V.tensor_tensor(out=xtmp[:], in0=x0p[0][:], in1=bcs(lim), op=ALU.mult), reads=[oTh[0], lim], writes=[oTh[2]])
                    K.op(DVE, lambda: V.tensor_tensor(out=lx0[:, 1], in0=lx0[:, 1], in1=xtmp[:], op=ALU.add), reads=[lx0, oTh[2]], writes=[lx0])

                wbd = [K.sb(f"wbd{i}", [128, 2, 128], BF16) for i in range(2)]
                wct = [K.sb(f"wct{i}", [128, 2, 128], BF16) for i in range(2)]

                def scol(b_, pos):
                    return b_[:, NPB:TB].rearrange("p (s t) -> p s t", t=4)[:, :, pos]

                fin = K.sb("s5fin", [128, 2, NS_TOK + 1], F32)
                _fm = K.sb("s5fm0", [128, NS_TOK + 1], F32)
                fm = [_fm, _fm]

                def s5_block(blk, full=True, trow=0):
                    K.dma(SP, tposb[:], tpos_in[trow:trow + 1, :].partition_broadcast(128), writes=[tposb])
                    ub = ybj
                    for jj in range(4):
                        slot = next_wu()
                        K.dma(POOL, slot[:], w_in_v[:, :, 4096 + jj * 256:4096 + (jj + 1) * 256], writes=[slot])
                        for hf in range(2):
                            j = 2 * jj + hf
                            a = next_acc()
                            project(a, slot, c0=hf * 128)
                            K.op(ACT, lambda: A.copy(out=ub[j][:], in_=a[:, 0:TB]), reads=[a], writes=[ub[j]])
                    glu_pref = []
                    if full:
                        for jj in range(2):
                            sl_ = next_wu()
                            K.dma(POOL, sl_[:, 0:8, :], gluw_v[:, :, jj * 256:(jj + 1) * 256], writes=[sl_])
                            glu_pref.append(sl_)
                        load_cb(0, 1)
                        opref["cb0"] = True
                    tabs = [(Fs[0], Fs[1], Fs[2]), (Fs[3], Fs[4], Fs[5])]
                    uins = [(Fs[6], Fs[7]), (Fs[8], oTh[0])]
                    tr, tn, m1a = oTh[1], oTh[2], oTh[3]
                    wr, wi = oTh[4], oTh[5]
                    yv, tg = oTh[6], oTh[7]
                    xb = [tuple(Bs[0:4]), tuple(Bs[0:4])]
                    csb, snb, nsnb, wrb, wib = Bs[4], Bs[5], Bs[6], Bs[7], Bs[8]
                    s0_, s1_ = blk * NSB, (blk + 1) * NSB
                    st = {}
                    c0 = NPB - 1

                    def stage_t(k):
                        cs, sn, nsn = tabs[k % 2]
                        K.op(ACT, lambda: A.activation(out=tr[:], in_=tposb[:], func=AF.Copy, scale=phif[:, k:k + 1]),
                             reads=[tposb, phif], writes=[tr])
                        K.op(DVE, lambda: V.tensor_scalar(out=tn[:], in0=tr[:], scalar1=MAGIC, scalar2=MAGIC,
                                                          op0=ALU.add, op1=ALU.subtract), reads=[tr], writes=[tn])
                        K.op(DVE, lambda: V.scalar_tensor_tensor(out=tr[:], in0=tn[:], scalar=-1.0, in1=tr[:],
                                                                 op0=ALU.mult, op1=ALU.add), reads=[tr, tn], writes=[tr])
                        K.op(ACT, lambda: A.activation(out=sn[:], in_=tr[:], func=AF.Sin, scale=TWO_PI), reads=[tr], writes=[sn])
                        K.op(ACT, lambda: A.activation(out=nsn[:], in_=tr[:], func=AF.Sin, scale=-TWO_PI), reads=[tr], writes=[nsn])
                        K.op(ACT, lambda: A.activation(out=tn[:], in_=tr[:], func=AF.Abs), reads=[tr], writes=[tn])
                        K.op(ACT, lambda: A.activation(out=cs[:], in_=tn[:], func=AF.Sin, scale=-TWO_PI, bias=halfpi[:, 0:1]),
                             reads=[tn, halfpi], writes=[cs])

                    def stage_a(k):
                        j = k // 4
                        cs, sn, nsn = tabs[k % 2]
                        ur, ui = uins[k % 2]
                        bd = wbd[k % 2]
                        ctw = wct[k % 2]
                        st[k] = (bd, ctw)
                        K.dma(SP, bd[:], bdt_d[k], reads=[bdt_b], writes=[bd])
                        K.dma(SP, ctw[:], ct_d[k], reads=[ct_b], writes=[ctw])
                        for ri in range(2):
                            a = acc[ri]
                            K.op(PE, lambda: T.matmul(a[:, 0:NPB], lhsT=bd[:, ri, :], rhs=ub[j][:, 0:NPB], start=True, stop=True),
                                 reads=[bd, ub[j]], writes=[a], inc=False)
                            K.op(PE, lambda: T.matmul(a[:, NPB:TB], lhsT=bd[:, ri, :], rhs=ub[j][:, NPB:TB], start=True, stop=True),
                                 reads=[bd, ub[j]], writes=[a])
                        K.op(DVE, lambda: V.tensor_tensor(out=ur[:], in0=acc[0][:, 0:TB], in1=cs[:], op=ALU.mult), reads=[acc[0], cs], writes=[ur])
                        K.op(DVE, lambda: V.tensor_tensor(out=m1a[:], in0=acc[1][:, 0:TB], in1=sn[:], op=ALU.mult), reads=[acc[1], sn], writes=[m1a])
                        K.op(POOL, lambda: G.tensor_tensor(out=ur[:], in0=ur[:], in1=m1a[:], op=ALU.add), reads=[ur, m1a], writes=[ur])
                        K.op(DVE, lambda: V.tensor_tensor(out=ui[:], in0=acc[1][:, 0:TB], in1=cs[:], op=ALU.mult), reads=[acc[1], cs], writes=[ui])
                        K.op(DVE, lambda: V.tensor_tensor(out=tn[:], in0=acc[0][:, 0:TB], in1=nsn[:], op=ALU.mult), reads=[acc[0], nsn], writes=[tn])
                        K.op(POOL, lambda: G.tensor_tensor(out=ui[:], in0=ui[:], in1=tn[:], op=ALU.add), reads=[ui, tn], writes=[ui])
                        if full:
                            K.op(DVE, lambda: V.tensor_tensor(out=scol(ur, 0), in0=scol(ur, 0), in1=lx0[:, 0, k, s0_:s1_], op=ALU.add),
                                 reads=[ur, lx0], writes=[ur])
                            K.op(DVE, lambda: V.tensor_tensor(out=scol(ui, 0), in0=scol(ui, 0), in1=lx0[:, 1, k, s0_:s1_], op=ALU.add),
                                 reads=[ui, lx0], writes=[ui])

                    def stage_s(k):
                        ur, ui = uins[k % 2]
                        magb = mag[:, k:k + 1].to_broadcast([128, NPB])
                        for (w_, u_, ri_) in ((wr, ur, 0), (wi, ui, 1)):
                            K.op(DVE, lambda: V.tensor_tensor_scan(out=w_[:, 0:NPB], data0=magb, data1=u_[:, 0:NPB],
                                                                   initial=carry[:, ri_, k:k + 1], op0=ALU.mult, op1=ALU.add),
                                 reads=[mag, u_, carry], writes=[w_])
                            if full:
                                K.op(DVE, lambda: V.tensor_tensor_scan(out=w_[:, NPB:TB], data0=rts[:, k, :], data1=u_[:, NPB:TB],
                                                                       initial=0.0, op0=ALU.mult, op1=ALU.add),
                                     reads=[rts, u_], writes=[w_])
                        K.op(ACT, lambda: A.copy(out=carry[:, 0, k:k + 1], in_=wr[:, NPB - 1:NPB]), reads=[wr], writes=[carry])
                        K.op(ACT, lambda: A.copy(out=carry[:, 1, k:k + 1], in_=wi[:, NPB - 1:NPB]), reads=[wi], writes=[carry])

                    def stage_r(k):
                        j = k // 4
                        kk = k % 4
                        cs, sn, nsn = tabs[k % 2]
                        p1, p2, p3, p4 = xb[k % 2]
                        bd, ctw = st.pop(k)
                        for (src_, dst_) in ((wr, wrb), (wi, wib), (cs, csb), (sn, snb), (nsn, nsnb)):
                            K.op(ACT, lambda: A.copy(out=dst_[:], in_=src_[:]), reads=[src_], writes=[dst_])
                        K.op(DVE, lambda: V.tensor_tensor(out=p1[:], in0=wrb[:], in1=csb[:], op=ALU.mult), reads=[wrb, csb], writes=[p1])
                        K.op(DVE, lambda: V.tensor_tensor(out=p2[:], in0=wib[:], in1=nsnb[:], op=ALU.mult), reads=[wib, nsnb], writes=[p2])
                        K.op(DVE, lambda: V.tensor_tensor(out=p3[:], in0=wib[:], in1=csb[:], op=ALU.mult), reads=[wib, csb], writes=[p3])
                        K.op(DVE, lambda: V.tensor_tensor(out=p4[:], in0=wrb[:], in1=snb[:], op=ALU.mult), reads=[wrb, snb], writes=[p4])
                        K.op(DVE, lambda: V.tensor_tensor(out=fin[:, 0, :], in0=wr[:, c0:TB], in1=cs[:, c0:TB], op=ALU.mult), reads=[wr, cs], writes=[fin])
                        K.op(DVE, lambda: V.tensor_tensor(out=fm[0][:], in0=wi[:, c0:TB], in1=nsn[:, c0:TB], op=ALU.mult), reads=[wi, nsn], writes=[fm[0]])
                        K.op(DVE, lambda: V.tensor_tensor(out=fin[:, 0, :], in0=fin[:, 0, :], in1=fm[0][:], op=ALU.add), reads=[fin, fm[0]], writes=[fin])
                        K.op(DVE, lambda: V.tensor_tensor(out=fin[:, 1, :], in0=wi[:, c0:TB], in1=cs[:, c0:TB], op=ALU.mult), reads=[wi, cs], writes=[fin])
                        K.op(DVE, lambda: V.tensor_tensor(out=fm[1][:], in0=wr[:, c0:TB], in1=sn[:, c0:TB], op=ALU.mult), reads=[wr, sn], writes=[fm[1]])
                        K.op(DVE, lambda: V.tensor_tensor(out=fin[:, 1, :], in0=fin[:, 1, :], in1=fm[1][:], op=ALU.add), reads=[fin, fm[1]], writes=[fin])
                        K.op(ACT, lambda: A.copy(out=xsf[:, :, k, :],
                                                 in_=fin[:, :, 1:1 + NS_TOK].rearrange("p r (s t) -> p r s t", t=4)[:, :, :, 3]),
                             reads=[fin], writes=[xsf])
                        if blk == nblk - 1:
                            K.op(ACT, lambda: A.copy(out=xfin[:, :, k:k + 1], in_=fin[:, :, 0:1]), reads=[fin], writes=[xfin])
                        prods = ((p1, 0), (p2, 0), (p3, 1), (p4, 1))
                        for (lo, hi, pbuf, pap) in ((0, NPB, pA, pA[:, 0:NPB]), (NPB, TB, pB, pB[:, 0:NS_TOK])):
                            for qi, (pp, ci) in enumerate(prods):
                                K.op(PE, lambda: T.matmul(pap, lhsT=ctw[:, ci, :], rhs=pp[:, lo:hi],
                                                          start=(kk == 0 and qi == 0), stop=(kk == 3 and qi == 3)),
                                     reads=[ctw, pp], writes=[pbuf], inc=(hi == TB and qi == 3))
                        if kk == 3:
                            K.op(DVE, lambda: V.scalar_tensor_tensor(out=yv[:, 0:NPB], in0=ub[j][:, 0:NPB], scalar=dcol[:, j:j + 1],
                                                                     in1=pA[:, 0:NPB], op0=ALU.mult, op1=ALU.add),
                                 reads=[ub[j], dcol, pA], writes=[yv])
                            K.op(DVE, lambda: V.scalar_tensor_tensor(out=yv[:, NPB:TB], in0=ub[j][:, NPB:TB], scalar=dcol[:, j:j + 1],
                                                                     in1=pB[:, 0:NS_TOK], op0=ALU.mult, op1=ALU.add),
                                 reads=[ub[j], dcol, pB], writes=[yv])
                            K.op(ACT, lambda: A.activation(out=tg[:], in_=yv[:], func=AF.Square), reads=[yv], writes=[tg])
                            K.op(POOL, lambda: G.tensor_scalar(out=tg[:], in0=tg[:], scalar1=0.044715, scalar2=1.0,
                                                               op0=ALU.mult, op1=ALU.add), reads=[tg], writes=[tg])
                            K.op(POOL, lambda: G.tensor_tensor(out=tg[:], in0=tg[:], in1=yv[:], op=ALU.mult), reads=[tg, yv], writes=[tg])
                            K.op(ACT, lambda: A.activation(out=tg[:], in_=tg[:], func=AF.Sigmoid, scale=2.0 * math.sqrt(2.0 / math.pi)),
                                 reads=[tg], writes=[tg])
                            K.op(DVE, lambda: V.tensor_tensor(out=ybj[j][:], in0=yv[:], in1=tg[:], op=ALU.mult), reads=[yv, tg], writes=[ybj[j]])

                    npair = dbg_npair
                    stage_t(0)
                    if npair > 1:
                        stage_t(1)
                    stage_a(0)
                    for k in range(npair):
                        stage_s(k)
                        if k + 1 < npair:
                            stage_a(k + 1)
                        if full:
                            stage_r(k)
                        else:
                            st.pop(k)
                        if k + 2 < npair:
                            stage_t(k + 2)
                    if not full:
                        return
                    for ri in range(2):
                        dst_ = (rs_o if ri == 0 else is_o)[blk * NSB:(blk + 1) * NSB].rearrange("s g p -> s (g p)")
                        for q in range(8):
                            stg_ = oTh[6 + q % 2]
                            for kk in range(4):
                                K.op(PE, lambda: T.transpose(out=pC[:NSB, kk * 128:(kk + 1) * 128], in_=xsf[:, ri, 4 * q + kk, :],
                                                             identity=identf[:, :]),
                                     reads=[xsf, identf], writes=[pC], inc=(kk == 3))
                            K.op(ACT, lambda: A.copy(out=stg_[:NSB, 0:512], in_=pC[:NSB, 0:512]), reads=[pC], writes=[stg_])
                            K.dma(SP, dst_[:, q * 512:(q + 1) * 512], stg_[:NSB, 0:512], reads=[stg_])
                    tg2 = oTh[7]
                    for jo in range(8):
                        if jo % 2 == 0:
                            if jo // 2 < len(glu_pref):
                                slot = glu_pref[jo // 2]
                            else:
                                slot = next_wu()
                                K.dma(POOL, slot[:, 0:8, :], gluw_v[:, :, jo * 128:(jo + 2) * 128], writes=[slot])
                        a = next_acc()
                        project(a, slot, nk=8, rhs=lambda kc, lo, hi: (ybj[kc], ybj[kc][:, lo:hi]), c0=(jo % 2) * 128)
                        K.op(ACT, lambda: A.activation(out=tg2[:], in_=a[:, 0:TB], func=AF.Sigmoid, bias=glub[:, jo:jo + 1]),
                             reads=[a, glub], writes=[tg2])
                        K.op(DVE, lambda: V.tensor_tensor(out=mixS[:, jo, :], in0=ybj[jo][:], in1=tg2[:], op=ALU.mult),
                             reads=[ybj[jo], tg2], writes=[mixS])

            flag = K.sb("flag_sb", [128, 1], F32)
            K.dma(SP, flag[:], flag_in, writes=[flag])
            blocks = [("pre", i) for i in range(npre)] + [("main", i) for i in range(nblk)]
            ngrp = 0 if dbg_nomlp else DFF // (128 * HG)
            pref = {}

            def load_wu_group(g_):
                sl = []
                for half in range(HG // 2):
                    w_ = next_wu()
                    col_ = (g_ * HG + 2 * half) * 128
                    K.dma(POOL, w_[:], mlp_up_v[:, :, col_:col_ + 256], writes=[w_])
                    sl.append(w_)
                return sl
            for bi, (mode, blk) in enumerate(blocks):
                full = mode == "main"
                src = xp if full else xpre
                for i in range(4):
                    K.dma(SP, xt[i][:], src[blk * NPB + i * 128: blk * NPB + (i + 1) * 128, :], writes=[xt[i]])
                if full:
                    K.dma(SP, xt[4][:NS_TOK, :], xs[blk * NS_TOK:(blk + 1) * NS_TOK, :], writes=[xt[4]])
                elif blk == 0:
                    K.op(DVE, lambda: V.memset(xt[4][:], 0.0), writes=[xt[4]])

                if stage != "dense":
                    norm_to_hT("g1")
                    hgrn_all(blk, full)
                    if full:
                        hgrn_finish()
                    if stage == "full":
                        s5_block(blk, full, bi)
                    if not full:
                        if blk == npre - 1:
                            K.op(DVE, lambda: V.tensor_scalar(out=Sst[:], in0=Sst[:], scalar1=flag[:, 0:1], scalar2=None, op0=ALU.mult),
                                 reads=[Sst, flag], writes=[Sst])
                            if stage == "full":
                                K.op(DVE, lambda: V.tensor_scalar(out=carry[:], in0=carry[:], scalar1=flag[:, 0:1], scalar2=None,
                                                                  op0=ALU.mult), reads=[carry, flag], writes=[carry])
                        continue
                    out_proj(hook=(lambda: pref.__setitem__("wu", load_wu_group(0))) if ngrp else None)
                    if blk == nblk - 1:
                        for h in range(8):
                            K.dma(SP, hp[h], Sst[:, h, :], reads=[Sst])

                norm_to_hT("g2")
                rl = [Fs[0], Fs[1]]
                if pref.get("wu") is None and ngrp:
                    pref["wu"] = load_wu_group(0)
                wu_next = pref.pop("wu", None)
                for g in range(ngrp):
                    wds = []
                    for hc in range(HG):
                        col = (g * HG + hc) * 128
                        wdsl = next_wd()
                        K.dma(POOL, wdsl[:], mlp_dn[col:col + 128, :], writes=[wdsl])
                        wds.append(wdsl)
                    wu_cur = wu_next
                    for hc in range(HG):
                        wus = wu_cur[hc // 2]
                        a = next_acc()
                        project(a, wus, c0=(hc % 2) * 128)
                        r = rl[hc % 2]
                        hb = ybj[(g % 2) * HG + hc]
                        K.op(ACT, lambda: A.activation(out=r[:, :], in_=a[:, 0:TB], func=AF.Relu), reads=[a], writes=[r])
                        K.op(DVE, lambda: V.tensor_tensor(out=hb[:, :], in0=r[:, :], in1=r[:, :], op=ALU.mult),
                             reads=[r], writes=[hb])
                    if g + 1 < ngrp:
                        wu_next = load_wu_group(g + 1)
                    for i in range(5):
                        n = tile_n[i]
                        o = tile_off[i]
                        for cb in range(4):
                            ad = next_acc()
                            for hc in range(HG):
                                hb = ybj[(g % 2) * HG + hc]
                                K.op(PE, lambda: T.matmul(ad[:n, 0:512], lhsT=hb[:, o:o + n],
                                                          rhs=wds[hc][:, cb * 512:(cb + 1) * 512],
                                                          start=(hc == 0), stop=(hc == HG - 1)),
                                     reads=[hb, wds[hc]], writes=[ad], inc=(hc == HG - 1))
                            K.op(DVE, lambda: V.tensor_tensor(out=xt[i][:n, cb * 512:(cb + 1) * 512],
                                                              in0=ad[:n, 0:512], in1=xt[i][:n, cb * 512:(cb + 1) * 512],
                                                              op=ALU.add),
                                 reads=[ad, xt[i]], writes=[xt[i]])
                load_gain("gf")
                rms_stats_all(mixS, mixS[:].rearrange("p j t -> p (j t)")[:, 0:D])
                for i in range(5):
                    n = tile_n[i]
                    K.op(DVE, lambda: V.scalar_tensor_tensor(out=xt[i][:n, :], in0=xt[i][:n, :],
                                                             scalar=rstd[:n, i:i + 1], in1=gb[:n, :],
                                                             op0=ALU.mult, op1=ALU.mult),
                         reads=[xt[i], rstd, gb], writes=[xt[i]])
                    if i < 4:
                        K.dma(SP, yp[blk * NPB + i * 128: blk * NPB + (i + 1) * 128, :], xt[i][:], reads=[xt[i]])
                    else:
                        K.dma(SP, ys[blk * NS_TOK:(blk + 1) * NS_TOK, :], xt[4][:NS_TOK, :], reads=[xt[4]])

            if stage == "full":
                for ri in range(2):
                    K.op(PE, lambda: T.transpose(out=pC[:32, ri * 128:(ri + 1) * 128], in_=xfin[:, ri, :], identity=identf[:, :]),
                         reads=[xfin, identf], writes=[pC], inc=(ri == 1))
                K.op(ACT, lambda: A.copy(out=oTh[6][:32, 0:256], in_=pC[:32, 0:256]), reads=[pC], writes=[oTh[6]])
                K.dma(SP, rp.rearrange("(k g2) p -> k (g2 p)", g2=2), oTh[6][:32, 0:128], reads=[oTh[6]])
                K.dma(SP, ip.rearrange("(k g2) p -> k (g2 p)", g2=2), oTh[6][:32, 128:256], reads=[oTh[6]])

            for s, v in K.out_tokens.values():
                SP.e.wait_ge(s, v)
            for e in (PE, ACT, DVE, POOL):
                if e.cnt:
                    SP.e.wait_ge(e.sem, e.cnt)
    return nc


def const_inputs(nblk=NBLK, npre=NPRE, half=0):
    c = {}
    c["ident"] = np.eye(128, dtype=np.float32)
    s_ = np.arange(64)
    c["maskP"] = (s_[:, None] <= s_[None, :]).astype(np.float32)
    t_ = np.arange(NS_TOK)
    c["maskS"] = ((t_[:, None] <= t_[None, :]) & (t_[:, None] // 4 == t_[None, :] // 4)).astype(np.float32)
    cm = np.ones((1, TB), np.float32)
    cm[0, 0:NPB:64] = 0.0
    cm[0, NPB:TB:4] = 0.0
    c["cmask"] = cm
    c["seqm"] = (t_[:, None] // 4 == np.arange(NSB)[None, :]).astype(np.float32)
    tp_ = np.zeros((nblk + npre, TB), np.float32)
    for b in range(npre):
        tp_[b, :NPB] = b * NPB + np.arange(NPB)
    for b in range(nblk):
        tp_[npre + b, :NPB] = (half * npre + b) * NPB + np.arange(NPB)
    tp_[:, NPB:] = np.arange(NS_TOK) % 4
    c["tpos"] = tp_
    rm = np.ones((1, TB), np.float32)
    rm[0, NPB:TB:4] = 0.0
    c["rmask"] = rm
    q = np.arange(128)
    c["hm"] = (q[:, None] // 64 == np.arange(2)[None, :]).astype(np.float32)
    c["gm"] = (q[:, None] // 16 == np.arange(8)[None, :]).astype(np.float32)
    c["flag"] = np.full((128, 1), float(half), np.float32)
    return c


def weight_inputs(inputs):
    f = lambda k: np.ascontiguousarray(np.asarray(inputs[k], dtype=np.float32))
    return dict(w_in=f("w_in")[0], w_out=f("w_out")[0], g1=f("norm1_g"), g2=f("norm2_g"),
                gf=f("final_norm_g")[None], mlp_up=f("mlp_up")[0], mlp_dn=f("mlp_down")[0],
                lbl=f("hgrn_lb_logits"), hgn=f("hgrn_norm_g"),
                a_re=f("s5_a_re")[0], a_im=f("s5_a_im")[0], lstep=f("s5_log_step"),
                b_re=f("s5_b_re")[0], b_im=f("s5_b_im")[0], c_re=f("s5_c_re")[0], c_im=f("s5_c_im")[0],
                s5d=f("s5_d"), glu_w=f("glu_w")[0], glu_b=f("glu_b"))


def kernel(**inputs):
    f = lambda k: np.ascontiguousarray(np.asarray(inputs[k], dtype=np.float32))
    xpr, xsm, sth, s5r, s5i = f("x_prompt"), f("x_sample"), f("state_hgrn"), f("state_s5_re"), f("state_s5_im")
    nc = build(nblk=NBLK, stage="full", npre=NPRE)
    wts = weight_inputs(inputs)
    HALF = NBLK * NPB
    in_maps = []
    for c in range(8):
        seq, half = c // 2, c % 2
        m = const_inputs(NBLK, NPRE, half)
        m.update(wts)
        m["xp"] = np.ascontiguousarray(xpr[seq, half * HALF:(half + 1) * HALF])
        m["xpre"] = np.ascontiguousarray(xpr[seq, 0:NPRE * NPB])
        m["xs"] = np.ascontiguousarray(xsm[16 * c:16 * c + 16].reshape(64, D))
        m["sh"] = np.ascontiguousarray(sth[0, 16 * c:16 * c + 16])
        m["sre"] = np.ascontiguousarray(s5r[0, 16 * c:16 * c + 16])
        m["sim"] = np.ascontiguousarray(s5i[0, 16 * c:16 * c + 16])
        in_maps.append(m)
    res = run_bass_kernel_spmd(nc, in_maps, core_ids=list(range(8))).results
    y_prompt = np.stack([np.concatenate([res[2 * q]["yp"], res[2 * q + 1]["yp"]]) for q in range(4)]).astype(np.float32)
    y_sample = np.concatenate([res[c]["ys"].reshape(16, 4, D) for c in range(8)]).astype(np.float32)
    hp = np.stack([res[2 * q + 1]["hp"] for q in range(4)])[None].astype(np.float32)
    rp = np.stack([res[2 * q + 1]["rp"] for q in range(4)])[None].astype(np.float32)
    ip = np.stack([res[2 * q + 1]["ip"] for q in range(4)])[None].astype(np.float32)
    hs = np.concatenate([res[c]["hs"] for c in range(8)])[None].astype(np.float32)
    rs = np.concatenate([res[c]["rs"] for c in range(8)])[None].astype(np.float32)
    is_ = np.concatenate([res[c]["is_"] for c in range(8)])[None].astype(np.float32)
    return (y_prompt, y_sample, hp, rp, ip, hs, rs, is_)
```
